# Optimizing a Trainium2 kernel written in Bass

```python
import jax
import jax.numpy as jnp
from jax import lax
import numpy as np

D_MODEL = 2048
BATCH = 2
SEQ = 8192
DEPTH = 4

N_A_LAYERS = DEPTH // 2
N_B_LAYERS = DEPTH - N_A_LAYERS
RMS_EPS = 1e-6
ROPE_BASE = 10000.0

RET_HEADS = 8
RET_QK_DIM = D_MODEL // RET_HEADS
RET_V_DIM = 2 * D_MODEL // RET_HEADS
RET_CHUNK = 128

MLA_HEADS = D_MODEL // 128
MLA_NOPE_DIM = 128
MLA_ROPE_DIM = 64
MLA_V_DIM = 128
MLA_KV_RANK = 512
MLA_Q_RANK = 3 * MLA_KV_RANK
ATTN_BLOCK = 128

D_FF = 4 * D_MODEL

kernel_name = "yoco_retention_mla_hybrid"


def rms_norm(x, gain):
    xf = x.astype(jnp.float32)
    y = xf * lax.rsqrt(jnp.mean(xf * xf, axis=-1, keepdims=True) + RMS_EPS)
    return (y * gain.astype(jnp.float32)).astype(x.dtype)


def rope_tables(positions, dim):
    inv_freq = 1.0 / (ROPE_BASE ** (jnp.arange(0, dim, 2, dtype=jnp.float32) / dim))
    ang = positions.astype(jnp.float32)[..., None] * inv_freq
    return jnp.cos(ang), jnp.sin(ang)


def apply_rope(x, cos, sin):
    half = x.shape[-1] // 2
    xf = x.astype(jnp.float32)
    x1, x2 = xf[..., :half], xf[..., half:]
    return jnp.concatenate([x1 * cos - x2 * sin, x2 * cos + x1 * sin], axis=-1).astype(x.dtype)


def retention(h, w_in, w_out, cos, sin):
    b, s, _ = h.shape
    H, dk, dv, C = RET_HEADS, RET_QK_DIM, RET_V_DIM, RET_CHUNK
    n_chunks = s // C
    proj = h @ w_in
    q, k, v, gate = jnp.split(proj, [H * dk, 2 * H * dk, 2 * H * dk + H * dv], axis=-1)
    q = apply_rope(q.reshape(b, s, H, dk), cos[:, :, None], sin[:, :, None]).astype(jnp.float32)
    k = apply_rope(k.reshape(b, s, H, dk), cos[:, :, None], sin[:, :, None]).astype(jnp.float32) * (dk ** -0.5)
    v = v.reshape(b, s, H, dv).astype(jnp.float32)

    def to_chunks(t):
        return t.reshape(b, n_chunks, C, H, t.shape[-1]).transpose(0, 3, 1, 2, 4)

    q, k, v = to_chunks(q), to_chunks(k), to_chunks(v)
    log_gamma = jnp.log1p(-jnp.exp2(-5.0 - jnp.arange(H, dtype=jnp.float32)))
    n = jnp.arange(C, dtype=jnp.float32)
    diff = n[:, None] - n[None, :]
    decay = jnp.where(diff >= 0, jnp.exp(jnp.maximum(diff, 0.0) * log_gamma[:, None, None]), 0.0)
    scores = jnp.einsum('bhcnd,bhcmd->bhcnm', q, k) * decay[None, :, None]
    o_intra = jnp.einsum('bhcnm,bhcme->bhcne', scores, v)

    xi = jnp.exp((n + 1.0)[None, :] * log_gamma[:, None])
    zeta = jnp.exp((C - 1.0 - n)[None, :] * log_gamma[:, None])
    chunk_decay = jnp.exp(C * log_gamma)[None, :, None, None]
    qs = (q * xi[None, :, None, :, None]).transpose(2, 0, 1, 3, 4)
    ks = (k * zeta[None, :, None, :, None]).transpose(2, 0, 1, 3, 4)
    vs = v.transpose(2, 0, 1, 3, 4)

    def step(state, inp):
        qc, kc, vc = inp
        out = jnp.einsum('bhnd,bhde->bhne', qc, state)
        state = chunk_decay * state + jnp.einsum('bhmd,bhme->bhde', kc, vc)
        return state, out

    state0 = jnp.zeros((b, H, dk, dv), jnp.float32)
    _, o_cross = lax.scan(step, state0, (qs, ks, vs))
    o = o_intra + o_cross.transpose(1, 2, 0, 3, 4)
    o = o.transpose(0, 2, 3, 1, 4).reshape(b, s, H, dv)
    o = o * lax.rsqrt(jnp.mean(o * o, axis=-1, keepdims=True) + RMS_EPS)
    o = o.reshape(b, s, H * dv).astype(h.dtype)
    return (jax.nn.silu(gate) * o) @ w_out


def mla_shared_kv(h, w_kv_down, kv_norm, w_kv_up, k_nope_norm, k_pe_norm, cos, sin):
    b, s, _ = h.shape
    down = h @ w_kv_down
    c_kv = rms_norm(down[..., :MLA_KV_RANK], kv_norm)
    k_pe = down[..., MLA_KV_RANK:]
    kv = (c_kv @ w_kv_up).reshape(b, s, MLA_HEADS, MLA_NOPE_DIM + MLA_V_DIM)
    k_nope = rms_norm(kv[..., :MLA_NOPE_DIM], k_nope_norm)
    v = kv[..., MLA_NOPE_DIM:]
    k_pe = apply_rope(rms_norm(k_pe, k_pe_norm), cos, sin)
    return k_nope, k_pe, v


def mla_attention(h, w_dq, q_norm, w_uq, q_nope_norm, q_pe_norm, w_o, k_nope, k_pe, v, cos, sin):
    b, s, _ = h.shape
    H = MLA_HEADS
    n_blocks = s // ATTN_BLOCK
    c_q = rms_norm(h @ w_dq, q_norm)
    q = (c_q @ w_uq).reshape(b, s, H, MLA_NOPE_DIM + MLA_ROPE_DIM)
    q_nope = rms_norm(q[..., :MLA_NOPE_DIM], q_nope_norm)
    q_pe = apply_rope(rms_norm(q[..., MLA_NOPE_DIM:], q_pe_norm), cos[:, :, None], sin[:, :, None])
    scale = (MLA_NOPE_DIM + MLA_ROPE_DIM) ** -0.5
    qn_b = q_nope.reshape(b, n_blocks, ATTN_BLOCK, H, MLA_NOPE_DIM).transpose(1, 0, 3, 2, 4)
    qp_b = q_pe.reshape(b, n_blocks, ATTN_BLOCK, H, MLA_ROPE_DIM).transpose(1, 0, 3, 2, 4)
    kpos = jnp.arange(s)

    def attend(args):
        i, qn, qp = args
        sc = jnp.einsum('bhqd,bkhd->bhqk', qn, k_nope) + jnp.einsum('bhqr,bkr->bhqk', qp, k_pe)
        sc = sc.astype(jnp.float32) * scale
        qpos = i * ATTN_BLOCK + jnp.arange(ATTN_BLOCK)
        sc = jnp.where(kpos[None, :] <= qpos[:, None], sc, -jnp.inf)
        p = jax.nn.softmax(sc, axis=-1).astype(v.dtype)
        return jnp.einsum('bhqk,bkhd->bqhd', p, v)

    o = lax.map(attend, (jnp.arange(n_blocks), qn_b, qp_b))
    o = o.transpose(1, 0, 2, 3, 4).reshape(b, s, H * MLA_V_DIM)
    return o @ w_o


def sq_relu_mlp(h, w1, w2):
    return jnp.square(jax.nn.relu(h @ w1)) @ w2


def setup_inputs(seed: int = 0) -> dict:
    key = jax.random.key(seed)
    ks = jax.random.split(key, 24)
    f32 = jnp.float32
    out_gain = (2.0 * DEPTH) ** -0.5

    def w(k, shape, fan_in, gain=1.0):
        return jax.random.normal(k, shape, f32) * (gain * fan_in ** -0.5)

    def g(k, shape):
        return 1.0 + 0.02 * jax.random.normal(k, shape, f32)

    nA, nB = N_A_LAYERS, N_B_LAYERS
    ret_in_width = 2 * RET_HEADS * RET_QK_DIM + 2 * RET_HEADS * RET_V_DIM
    x = jax.random.normal(ks[0], (BATCH, SEQ, D_MODEL), f32)
    offset = jax.random.randint(ks[1], (BATCH, 1), 0, 1024, dtype=jnp.int32)
    positions = offset + jnp.arange(SEQ, dtype=jnp.int32)[None, :]
    return {
        "x": x,
        "positions": positions,
        "norm_mix": g(ks[2], (DEPTH, D_MODEL)),
        "norm_mlp": g(ks[3], (DEPTH, D_MODEL)),
        "ret_w_in": w(ks[4], (nA, D_MODEL, ret_in_width), D_MODEL),
        "ret_w_out": w(ks[5], (nA, RET_HEADS * RET_V_DIM, D_MODEL), RET_HEADS * RET_V_DIM, out_gain),
        "kv_norm_in": g(ks[6], (D_MODEL,)),
        "mla_w_kv_down": w(ks[7], (D_MODEL, MLA_KV_RANK + MLA_ROPE_DIM), D_MODEL),
        "mla_kv_norm": g(ks[8], (MLA_KV_RANK,)),
        "mla_w_kv_up": w(ks[9], (MLA_KV_RANK, MLA_HEADS * (MLA_NOPE_DIM + MLA_V_DIM)), MLA_KV_RANK),
        "mla_k_nope_norm": g(ks[10], (MLA_NOPE_DIM,)),
        "mla_k_pe_norm": g(ks[11], (MLA_ROPE_DIM,)),
        "mla_w_dq": w(ks[12], (nB, D_MODEL, MLA_Q_RANK), D_MODEL),
        "mla_q_norm": g(ks[13], (nB, MLA_Q_RANK)),
        "mla_w_uq": w(ks[14], (nB, MLA_Q_RANK, MLA_HEADS * (MLA_NOPE_DIM + MLA_ROPE_DIM)), MLA_Q_RANK),
        "mla_q_nope_norm": g(ks[15], (nB, MLA_NOPE_DIM)),
        "mla_q_pe_norm": g(ks[16], (nB, MLA_ROPE_DIM)),
        "mla_w_o": w(ks[17], (nB, MLA_HEADS * MLA_V_DIM, D_MODEL), MLA_HEADS * MLA_V_DIM, out_gain),
        "mlp_w1": w(ks[18], (DEPTH, D_MODEL, D_FF), D_MODEL),
        "mlp_w2": w(ks[19], (DEPTH, D_FF, D_MODEL), D_FF, out_gain),
    }


def reference(x, positions, norm_mix, norm_mlp, ret_w_in, ret_w_out, kv_norm_in,
              mla_w_kv_down, mla_kv_norm, mla_w_kv_up, mla_k_nope_norm, mla_k_pe_norm,
              mla_w_dq, mla_q_norm, mla_w_uq, mla_q_nope_norm, mla_q_pe_norm, mla_w_o,
              mlp_w1, mlp_w2):
    ret_cos, ret_sin = rope_tables(positions, RET_QK_DIM)
    mla_cos, mla_sin = rope_tables(positions, MLA_ROPE_DIM)
    k_nope = k_pe = v = None
    for layer in range(DEPTH):
        if layer < N_A_LAYERS:
            h = rms_norm(x, norm_mix[layer])
            x = x + retention(h, ret_w_in[layer], ret_w_out[layer], ret_cos, ret_sin)
        else:
            j = layer - N_A_LAYERS
            if j == 0:
                k_nope, k_pe, v = mla_shared_kv(rms_norm(x, kv_norm_in), mla_w_kv_down, mla_kv_norm,
                                                mla_w_kv_up, mla_k_nope_norm, mla_k_pe_norm,
                                                mla_cos, mla_sin)
            h = rms_norm(x, norm_mix[layer])
            x = x + mla_attention(h, mla_w_dq[j], mla_q_norm[j], mla_w_uq[j], mla_q_nope_norm[j],
                                  mla_q_pe_norm[j], mla_w_o[j], k_nope, k_pe, v, mla_cos, mla_sin)
        x = x + sq_relu_mlp(rms_norm(x, norm_mlp[layer]), mlp_w1[layer], mlp_w2[layer])
    return x
```

```python
import math
from contextlib import ExitStack

import numpy as np
import concourse.bass as bass
import concourse.mybir as mybir
from concourse.bass_utils import run_bass_kernel_spmd

F32 = mybir.dt.float32
BF16 = mybir.dt.bfloat16
I32 = mybir.dt.int32
AF = mybir.ActivationFunctionType
ALU = mybir.AluOpType
AX = mybir.AxisListType

D = 2048
DFF = 8192
NCORE = 8
TOK = 2048
T = 512
EPS = 1e-6

COMPUTE = ("pe", "act", "dve", "pool")
DMAQ = ("sp", "act", "pool")
NDS = 8


class Op:
    __slots__ = ("eng", "fn", "deps", "signal", "sigval", "is_dma", "dma_n", "idx")

    def __init__(self, eng, fn, is_dma):
        self.eng = eng
        self.fn = fn
        self.deps = []
        self.signal = False
        self.sigval = 0
        self.is_dma = is_dma
        self.dma_n = -1
        self.idx = -1


class Sched:
    def __init__(self, nc, es):
        self.nc = nc
        self.csem = {e: es.enter_context(nc.semaphore("c_" + e)) for e in COMPUTE}
        self.ccount = {e: 0 for e in COMPUTE}
        self.dsem = {q: [es.enter_context(nc.semaphore("d_%s%d" % (q, i))) for i in range(NDS)]
                     for q in DMAQ}
        self.dcount = {q: 0 for q in DMAQ}
        self.known = {e: {} for e in ("pe", "act", "dve", "pool", "sp")}
        self.reset_phase()

    def reset_phase(self):
        self.ops = {e: [] for e in ("pe", "act", "dve", "pool", "sp")}
        self.lastw = {}
        self.readers = {}

    def op(self, eng, fn, reads=(), writes=(), dma=False):
        o = Op(eng, fn, dma)
        if dma:
            o.dma_n = self.dcount[eng]
            self.dcount[eng] += 1
        deps = {}
        for t in reads:
            w = self.lastw.get(t)
            if w is not None:
                deps[id(w)] = w
        for t in writes:
            w = self.lastw.get(t)
            if w is not None:
                deps[id(w)] = w
            for r in self.readers.get(t, ()):
                deps[id(r)] = r
        for d in deps.values():
            if d is o:
                continue
            if d.eng == "pe" and eng == "pe" and not d.is_dma and not dma:
                continue
            o.deps.append(d)
            if not d.is_dma:
                d.signal = True
        for t in reads:
            lst = self.readers.setdefault(t, [])
            if not dma:
                lst[:] = [r for r in lst if r.is_dma or r.eng != eng]
            lst.append(o)
        for t in writes:
            self.lastw[t] = o
            self.readers[t] = []
        o.idx = len(self.ops[eng])
        self.ops[eng].append(o)
        return o

    def dma(self, q, out, in_, reads=(), writes=()):
        return self.op(q, lambda e: e.dma_start(out=out, in_=in_), reads, writes, dma=True)

    def _dma_wait(self, d):
        return (self.dsem[d.eng][d.dma_n % NDS], 16 * (d.dma_n // NDS + 1))

    def emit(self, name=None):
        nc = self.nc
        for e in COMPUTE:
            cops = [o for o in self.ops[e] if not o.is_dma]
            if cops:
                cops[-1].signal = True
        for e in COMPUTE:
            for o in self.ops[e]:
                if not o.is_dma and o.signal:
                    self.ccount[e] += 1
                    o.sigval = self.ccount[e]
        final_c = dict(self.ccount)
        final_d = dict(self.dcount)

        def run(ename, eng):
            known = self.known[ename]

            def wait(sem, val, key):
                if known.get(key, 0) >= val:
                    return
                known[key] = val
                eng.wait_ge(sem, val)

            for o in self.ops[ename]:
                for d in o.deps:
                    if d.is_dma:
                        sem, val = self._dma_wait(d)
                        wait(sem, val, ("d", d.eng, d.dma_n % NDS))
                    else:
                        wait(self.csem[d.eng], d.sigval, ("c", d.eng))
                if o.is_dma:
                    slot = o.dma_n % NDS
                    if o.dma_n >= NDS:
                        wait(self.dsem[ename][slot], 16 * (o.dma_n // NDS), ("d", ename, slot))
                    o.fn(eng).then_inc(self.dsem[ename][slot], 16)
                else:
                    ins = o.fn(eng)
                    if o.signal:
                        ins.then_inc(self.csem[ename], 1)
            for e in COMPUTE:
                if final_c[e] > 0:
                    wait(self.csem[e], final_c[e], ("c", e))
            for q in DMAQ:
                n = final_d[q]
                for slot in range(NDS):
                    if n > slot:
                        cnt = (n - 1 - slot) // NDS + 1
                        wait(self.dsem[q][slot], 16 * cnt, ("d", q, slot))

        with nc.Block() as block:
            @block.tensor
            def _(eng):
                run("pe", eng)

            @block.scalar
            def _(eng):
                run("act", eng)

            @block.vector
            def _(eng):
                run("dve", eng)

            @block.gpsimd
            def _(eng):
                run("pool", eng)

            @block.sync
            def _(eng):
                run("sp", eng)
        self.reset_phase()


def cast_op(S, eng, out, in_, reads, writes):
    if eng == "act":
        return S.op("act", lambda e: e.copy(out, in_), reads, writes)
    return S.op(eng, lambda e: e.tensor_copy(out, in_), reads, writes)


KC = D // 128
FC = DFF // 128
TWO_PI = 2.0 * math.pi
CW1 = 6.28125
CW2 = TWO_PI - CW1


class Phase:
    uid = [0]

    def __init__(self, nc, S, cst):
        self.nc, self.S, self.cst = nc, S, cst
        self.es = ExitStack()
        Phase.uid[0] += 1
        self.sfx = "_p%d" % Phase.uid[0]
        self.wcount = 0
        self.ident = self.sb("ident", [128, 128], BF16)
        identf = self.sb("identf", [128, 128], F32)
        S.dma("sp", identf[:], cst["ident"], writes=["identf"])
        cast_op(S, "dve", self.ident[:], identf[:], ["identf"], ["ident"])

    def sb(self, name, shape, dt):
        return self.es.enter_context(self.nc.sbuf_tensor(name + self.sfx, shape, dt))

    def psum(self, name, shape, dt):
        return self.es.enter_context(self.nc.psum_tensor(name + self.sfx, shape, dt))

    def close(self):
        self.S.emit()
        self.es.close()

    def setup_stream(self, n_elem=4096):
        self.wst = [self.sb("wst%d" % i, [128, n_elem], F32) for i in range(2)]
        self.wbf = [self.sb("wbf%d" % i, [128, n_elem], BF16) for i in range(2)]

    def load_w(self, pieces, a, bdim, cast_engs=("act", "pool")):
        S = self.S
        i = self.wcount
        self.wcount += 1
        b = i % 2
        stv = self.wst[b][:, 0:a * bdim].rearrange("p (a b) -> p a b", a=a)
        bfv = self.wbf[b][:, 0:a * bdim].rearrange("p (a b) -> p a b", a=a)
        for (off, n, src) in pieces:
            S.dma("sp", stv[:, :, off:off + n], src, reads=[("wst", b)], writes=[("wst", b)])
        ce = cast_engs[i % len(cast_engs)]
        cast_op(S, ce, bfv, stv, [("wst", b)], [("wbf", b)])
        return bfv, ("wbf", b)

    def setup_norm(self, g_row):
        self.xt = self.sb("xt", [128, D], F32)
        self.hn = self.sb("hn", [128, D], BF16)
        self.gbc = self.sb("gbc", [128, D], F32)
        self.nst = self.sb("nst", [128, 2], F32)
        self.S.dma("sp", self.gbc[:], g_row.partition_broadcast(128), writes=["gbc"])

    def norm_hT(self, x_chunk, hT, c, ps_bf, KCn=KC):
        S, xt, hn, gbc, st = self.S, self.xt, self.hn, self.gbc, self.nst
        S.dma("sp", xt[:], x_chunk, writes=["xt"])
        S.op("act", lambda e: e.activation(hn[:], xt[:], AF.Square, accum_out=st[:, 0:1]),
             reads=["xt"], writes=["hn", "nst0"])
        self.rstd(st[:, 0:1], st[:, 1:2], 1.0 / D, EPS, "nst0", "nst1")
        S.op("dve", lambda e: e.scalar_tensor_tensor(hn[:], xt[:], st[:, 1:2], gbc[:], ALU.mult, ALU.mult),
             reads=["xt", "nst1", "gbc"], writes=["hn"])
        self.transpose_to(hn, KCn, lambda k0, n: hT[:, k0:k0 + n, c * 128:(c + 1) * 128], ps_bf,
                          ["hn"], [("hT", c)])

    def rstd(self, ssq, out, scale, eps, rtok, wtok):
        S = self.S
        if isinstance(eps, float):
            S.op("dve", lambda e: e.tensor_scalar(out, ssq, scale, eps, ALU.mult, ALU.add),
                 reads=[rtok], writes=[wtok])
        else:
            S.op("dve", lambda e: e.scalar_tensor_tensor(out, ssq, scale, eps, ALU.mult, ALU.add),
                 reads=[rtok, "rc"], writes=[wtok])
        S.op("act", lambda e: e.sqrt(out, out), reads=[wtok], writes=[wtok])
        S.op("dve", lambda e: e.reciprocal(out, out), reads=[wtok], writes=[wtok])

    def transpose_to(self, src, nblk, dst_fn, ps_bf, reads, writes, evac="act"):
        S = self.S
        k0 = 0
        while k0 < nblk:
            n = min(8, nblk - k0)
            bank = self.tb = (getattr(self, "tb", 0) + 1) % 2
            for k in range(n):
                S.op("pe", lambda e, k=k, k0=k0, bank=bank: e.transpose(
                    ps_bf[:, bank, k * 128:(k + 1) * 128], src[:, (k0 + k) * 128:(k0 + k + 1) * 128],
                    self.ident[:]), reads=list(reads) + ["ident"], writes=[("psb", bank)])
            dst = dst_fn(k0, n)
            srcv = ps_bf[:, bank, 0:n * 128].rearrange("p (k n) -> p k n", k=n)
            if evac == "act":
                S.op("act", lambda e, dst=dst, srcv=srcv: e.copy(dst, srcv), reads=[("psb", bank)], writes=writes)
            else:
                S.op(evac, lambda e, dst=dst, srcv=srcv: e.tensor_copy(dst, srcv), reads=[("psb", bank)],
                     writes=writes)
            k0 += n

    def rope_tables(self, pos_d, invf_d, nfreq, nchunk, name):
        S = self.S
        gsz = max(1, min(nchunk, 512 // nfreq))
        W = gsz * nfreq
        posi = self.sb(name + "posi", [128, nchunk], I32)
        posf = self.sb(name + "posf", [128, nchunk], F32)
        invf = self.sb(name + "invf", [128, nfreq], F32)
        ang = self.sb(name + "ang", [128, gsz, nfreq], F32)
        tmp = self.sb(name + "tmp", [128, W], F32)
        ki = self.sb(name + "ki", [128, W], I32)
        kf = self.sb(name + "kf", [128, W], F32)
        cos = self.sb(name + "cos", [128, nchunk, nfreq], F32)
        sin = self.sb(name + "sin", [128, nchunk, nfreq], F32)
        tk = name + "rt"
        S.dma("sp", posi[:], pos_d, writes=[tk + "posi"])
        S.dma("sp", invf[:], invf_d, writes=[tk + "invf"])
        S.op("dve", lambda e: e.tensor_copy(posf[:], posi[:]), [tk + "posi"], [tk + "posf"])
        angf = ang[:].rearrange("p c f -> p (c f)")
        for g0 in range(0, nchunk, gsz):
            for c in range(gsz):
                S.op("dve", lambda e, c=c, g0=g0: e.tensor_scalar(ang[:, c, :], invf[:], posf[:, g0 + c:g0 + c + 1],
                                                                  None, ALU.mult),
                     [tk + "posf", tk + "invf"], [tk + "ang"])
            for (dst, shift, wt) in ((sin, 0.0, tk + "sin"), (cos, 0.25, tk + "cos")):
                dstf = dst[:, g0:g0 + gsz, :].rearrange("p c f -> p (c f)")
                S.op("dve", lambda e, shift=shift: e.tensor_scalar(tmp[:], angf, 1.0 / TWO_PI, shift,
                                                                   ALU.mult, ALU.add),
                     [tk + "ang"], [tk + "tmp"])
                S.op("dve", lambda e: e.tensor_copy(ki[:], tmp[:]), [tk + "tmp"], [tk + "ki"])
                S.op("dve", lambda e: e.tensor_copy(kf[:], ki[:]), [tk + "ki"], [tk + "kf"])
                S.op("dve", lambda e: e.scalar_tensor_tensor(tmp[:], kf[:], -CW1, angf, ALU.mult, ALU.add),
                     [tk + "kf", tk + "ang"], [tk + "tmp"])
                S.op("dve", lambda e: e.scalar_tensor_tensor(tmp[:], kf[:], -CW2, tmp[:], ALU.mult, ALU.add),
                     [tk + "kf", tk + "tmp"], [tk + "tmp"])
                S.op("dve", lambda e, shift=shift: e.tensor_scalar(tmp[:], tmp[:], shift * TWO_PI, math.pi,
                                                                   ALU.add, ALU.min),
                     [tk + "tmp"], [tk + "tmp"])
                S.op("dve", lambda e: e.tensor_scalar(tmp[:], tmp[:], -math.pi, None, ALU.max),
                     [tk + "tmp"], [tk + "tmp"])
                S.op("act", lambda e, dstf=dstf: e.activation(dstf, tmp[:], AF.Sin), [tk + "tmp"], [wt])
        return cos, sin, tk + "cos", tk + "sin"


def mlp_phase(nc, S, cst, x_in, x_out, g_row, w1, w2, ntiles=TOK // T):
    W1B = 256
    P = Phase(nc, S, cst)
    P.setup_stream(4096)
    P.setup_norm(g_row)
    xr = P.sb("xr", [128, 4, 512], F32)
    hT = P.sb("hT", [128, KC, T], BF16)
    hid = P.sb("hid", [128, FC, T], BF16)
    rl = [P.sb("rl%d" % i, [128, T], F32) for i in range(2)]
    ps = P.psum("ps", [128, 6, 512], F32)
    ps_bf = P.psum("ps_bf", [128, 2, 1024], BF16)
    w1v = w1.rearrange("(kc p) n -> p kc n", p=128)
    w2v = w2.rearrange("(fc p) n -> p fc n", p=128)
    xin_v = x_in.rearrange("(t c p) d -> t c p d", p=128, c=4)
    xin_r = x_in.rearrange("(t c p) d -> t p c d", p=128, c=4)
    xout_r = x_out.rearrange("(t c p) d -> t p c d", p=128, c=4)
    for t in range(ntiles):
        for c in range(4):
            P.norm_hT(xin_v[t, c], hT, c, ps_bf)
        for b in range(DFF // W1B):
            wv, wtok = P.load_w([(0, W1B, w1v[:, :, b * W1B:(b + 1) * W1B])], KC, W1B)
            for oc in range(W1B // 128):
                bank = (b * 2 + oc) % 2 + 4
                for kc in range(KC):
                    S.op("pe", lambda e, kc=kc, oc=oc, bank=bank, wv=wv: e.matmul(
                        ps[:, bank, :], wv[:, kc, oc * 128:(oc + 1) * 128], hT[:, kc, :],
                        start=(kc == 0), stop=(kc == KC - 1)),
                        reads=[wtok] + [("hT", c) for c in range(4)], writes=[("ps", bank)])
                fc = b * (W1B // 128) + oc
                rb = fc % 2
                S.op("act", lambda e, rb=rb, bank=bank: e.activation(rl[rb][:], ps[:, bank, :], AF.Relu),
                     reads=[("ps", bank)], writes=[("rl", rb)])
                S.op("dve", lambda e, fc=fc, rb=rb: e.tensor_tensor(hid[:, fc, :], rl[rb][:], rl[rb][:], ALU.mult),
                     reads=[("rl", rb)], writes=[("hid", fc)])
        for dg in range(4):
            S.dma("sp", xr[:], xin_r[t][:, :, dg * 512:(dg + 1) * 512], writes=["xr"])
            for fb in range(8):
                wv, wtok = P.load_w([(0, 512, w2v[:, fb * 8:(fb + 1) * 8, dg * 512:(dg + 1) * 512])], 8, 512)
                for j in range(8):
                    fc = fb * 8 + j
                    for c in range(4):
                        S.op("pe", lambda e, fc=fc, j=j, c=c, wv=wv: e.matmul(
                            ps[:, c, :], hid[:, fc, c * 128:(c + 1) * 128], wv[:, j, :],
                            start=(fc == 0), stop=(fc == FC - 1)),
                            reads=[wtok, ("hid", fc)], writes=[("ps", c)])
            for c in range(4):
                S.op("dve", lambda e, c=c: e.tensor_tensor(xr[:, c, :], xr[:, c, :], ps[:, c, :], ALU.add),
                     reads=[("ps", c), "xr"], writes=["xr"])
            S.dma("sp", xout_r[t][:, :, dg * 512:(dg + 1) * 512], xr[:], reads=["xr"], writes=[("xout", t, dg)])
    P.close()


RH, RDK, RDV = 8, 256, 512


def ret_consts():
    lg = np.log1p(-np.exp2(-5.0 - np.arange(RH, dtype=np.float64)))
    n = np.arange(128, dtype=np.float64)
    ginv = np.exp(-(n[:, None] + 1.0) * lg[None, :]) / 16.0
    zeta = np.exp((127.0 - n[:, None]) * lg[None, :]) / 16.0
    epsx = EPS / np.exp(2.0 * (n[:, None] + 1.0) * lg[None, :])
    rc = np.concatenate([ginv, zeta, epsx], axis=1).astype(np.float32)
    cd = [float(np.exp(128.0 * v)) for v in lg]
    mask = (n[None, :] >= n[:, None]).astype(np.float32)
    return rc, cd, mask, lg


def ret_phase(nc, S, cst, mode, x_in, x_out, g_row, w_in, w_out, pos, s_all, coef, s_out, nrank,
              ntiles=TOK // T):
    rc_np, cd, _, _ = ret_consts()
    P = Phase(nc, S, cst)
    P.setup_stream(4096)
    P.setup_norm(g_row)
    hT = P.sb("hT", [128, KC, T], BF16)
    St = P.sb("St", [128, RH, 2, 512], F32)
    Sb = P.sb("Sb", [128, 2, 512], BF16)
    rc = P.sb("rc", [128, 24], F32)
    mask = P.sb("mask", [128, 128], F32)
    qk_f = P.sb("qk_f", [128, 4, 512], F32)
    qk_b = P.sb("qk_b", [128, 4, 512], BF16)
    kz = P.sb("kz", [128, 256], BF16)
    qkT = P.sb("qkT", [128, 4, 128], BF16)
    v_b = P.sb("v_b", [128, 4, 512], BF16)
    ra = P.sb("ra", [128, 2, 128], F32)
    rb_ = P.sb("rb_", [128, 2, 128], F32)
    ps = P.psum("ps", [128, 6, 512], F32)
    ps_bf = P.psum("ps_bf", [128, 2, 1024], BF16)
    S.dma("sp", rc[:], cst["rc"], writes=["rc"])
    S.dma("sp", mask[:], cst["mask"], writes=["mask"])
    if mode == "B":
        goT = P.sb("goT", [128, 32, T], BF16)
        sg = P.sb("sg", [128, 4, 512], BF16)
        PT = P.sb("PT", [128, 128], BF16)
        go = P.sb("go", [128, 512], BF16)
        gst = P.sb("gst", [128, 2], F32)
        xr = P.xt[:].rearrange("p (c n) -> p c n", c=4)
        cf = P.sb("cf", [128, nrank * RH], F32)
        stmp = P.sb("stmp", [128, 2, 512], F32)
    cos, sin, ctok, stok = P.rope_tables(pos, cst["invf_ret"], 128, TOK // 128, "r")

    S.op("pool", lambda e: e.memset(St[:].rearrange("p h a b -> p (h a b)"), 0.0), [], ["St"])
    if mode == "B":
        S.dma("sp", cf[:], coef.partition_broadcast(128), writes=["cf"])
        for r in range(nrank):
            for h in range(RH):
                S.dma("sp", stmp[:], s_all[r, h].rearrange("(a p) e -> p a e", p=128), writes=["stmp"])
                S.op("dve", lambda e, r=r, h=h: e.scalar_tensor_tensor(
                    St[:, h].rearrange("p a b -> p (a b)"), stmp[:].rearrange("p a b -> p (a b)"),
                    cf[:, r * RH + h:r * RH + h + 1], St[:, h].rearrange("p a b -> p (a b)"), ALU.mult, ALU.add),
                    reads=["stmp", "cf", "St"], writes=["St"])

    w_in_v = w_in.rearrange("(kc p) n -> p kc n", p=128)
    xin_v = x_in.rearrange("(t c p) d -> t c p d", p=128, c=4)
    QO, KO, VO, GO = 0, RH * RDK, 2 * RH * RDK, 2 * RH * RDK + RH * RDV

    def proj256(col0, evac):
        wv, wtok = P.load_w([(0, 256, w_in_v[:, :, col0:col0 + 256])], KC, 256)
        for c in range(4):
            bank = 4 + (P.wcount * 4 + c) % 2
            for kc in range(KC):
                S.op("pe", lambda e, kc=kc, c=c, bank=bank, wv=wv: e.matmul(
                    ps[:, bank, 0:256], hT[:, kc, c * 128:(c + 1) * 128], wv[:, kc, :],
                    start=(kc == 0), stop=(kc == KC - 1)),
                    reads=[wtok, ("hT", c)], writes=[("ps", bank)])
            evac(c, ps[:, bank, 0:256], ("ps", bank))

    for t in range(ntiles):
        for c in range(4):
            P.norm_hT(xin_v[t, c], hT, c, ps_bf)
        for h in range(RH):
            if mode == "B":
                proj256(QO + h * RDK, lambda c, p, tk: S.op(
                    "act", lambda e: e.copy(qk_f[:, c, 0:256], p), reads=[tk], writes=[("qk_f", c)]))
            proj256(KO + h * RDK, lambda c, p, tk: S.op(
                "act", lambda e: e.copy(qk_f[:, c, 256:512], p), reads=[tk], writes=[("qk_f", c)]))
            for half in range(2):
                proj256(VO + h * RDV + half * 256, lambda c, p, tk, half=half: S.op(
                    "act", lambda e: e.copy(v_b[:, c, half * 256:(half + 1) * 256], p),
                    reads=[tk], writes=[("v_b", c)]))
            if mode == "B":
                for half in range(2):
                    proj256(GO + h * RDV + half * 256, lambda c, p, tk, half=half: S.op(
                        "act", lambda e: e.activation(sg[:, c, half * 256:(half + 1) * 256], p, AF.Silu),
                        reads=[tk], writes=[("sg", c)]))
            S.op("pool", lambda e, h=h: e.tensor_copy(Sb[:], St[:, h]), reads=["St"], writes=["Sb"])
            for c in range(4):
                j = t * 4 + c
                xv = qk_f[:, c, :].rearrange("p (a b f) -> p a b f", a=2, b=2)
                ov = qk_b[:, c, :].rearrange("p (a b f) -> p a b f", a=2, b=2)
                cb = cos[:, j, :].unsqueeze(1).broadcast_to([128, 2, 128])
                sbb = sin[:, j, :].unsqueeze(1).broadcast_to([128, 2, 128])
                rd = [("qk_f", c), ctok, stok]
                S.op("dve", lambda e, xv=xv, cb=cb: e.tensor_tensor(ra[:], xv[:, :, 0, :], cb, ALU.mult), rd, ["ra"])
                S.op("pool", lambda e, xv=xv, sbb=sbb: e.tensor_tensor(rb_[:], xv[:, :, 1, :], sbb, ALU.mult), rd, ["rb"])
                S.op("dve", lambda e, ov=ov: e.tensor_tensor(ov[:, :, 0, :], ra[:], rb_[:], ALU.subtract),
                     ["ra", "rb"], [("qk_b", c)])
                S.op("dve", lambda e, xv=xv, cb=cb: e.tensor_tensor(ra[:], xv[:, :, 1, :], cb, ALU.mult), rd, ["ra"])
                S.op("pool", lambda e, xv=xv, sbb=sbb: e.tensor_tensor(rb_[:], xv[:, :, 0, :], sbb, ALU.mult), rd, ["rb"])
                S.op("dve", lambda e, ov=ov: e.tensor_tensor(ov[:, :, 1, :], ra[:], rb_[:], ALU.add),
                     ["ra", "rb"], [("qk_b", c)])
                S.op("pool", lambda e, c=c, h=h: e.tensor_scalar(kz[:], qk_b[:, c, 256:512], rc[:, 8 + h:9 + h], None,
                                                                  ALU.mult), [("qk_b", c), "rc"], ["kz"])
                if mode == "B":
                    P.transpose_to(qk_b[:, c, :], 4, lambda k0, n: qkT[:, k0:k0 + n, :], ps_bf,
                                   [("qk_b", c)], ["qkT"], evac="act")
                    for dc in range(2):
                        S.op("pe", lambda e, dc=dc: e.matmul(ps[:, 0, 0:128], qkT[:, 2 + dc, :], qkT[:, dc, :],
                                                             start=(dc == 0), stop=(dc == 1)),
                             reads=["qkT"], writes=[("ps", 0)])
                    S.op("dve", lambda e, h=h: e.scalar_tensor_tensor(PT[:], ps[:, 0, 0:128], rc[:, h:h + 1], mask[:],
                                                                      ALU.mult, ALU.mult),
                         reads=[("ps", 0), "rc", "mask"], writes=["PT"])
                    for dc in range(2):
                        S.op("pe", lambda e, dc=dc: e.matmul(ps[:, 1, :], qkT[:, dc, :], Sb[:, dc, :],
                                                             start=(dc == 0), stop=False),
                             reads=["qkT", "Sb"], writes=[("ps", 1)])
                    S.op("pe", lambda e, c=c: e.matmul(ps[:, 1, :], PT[:], v_b[:, c, :], start=False, stop=True),
                         reads=["PT", ("v_b", c)], writes=[("ps", 1)])
                for dc in range(2):
                    S.op("pe", lambda e, dc=dc, c=c: e.matmul(ps[:, 2 + dc, :], kz[:, dc * 128:(dc + 1) * 128],
                                                              v_b[:, c, :], start=True, stop=True),
                         reads=["kz", ("v_b", c)], writes=[("ps", 2 + dc)])
                    S.op("dve", lambda e, dc=dc, h=h: e.scalar_tensor_tensor(
                        St[:, h, dc, :], St[:, h, dc, :], cd[h], ps[:, 2 + dc, :], ALU.mult, ALU.add),
                        reads=[("ps", 2 + dc), "St"], writes=["St"])
                if c < 3 and mode == "B":
                    S.op("pool", lambda e, h=h: e.tensor_copy(Sb[:], St[:, h]), reads=["St"], writes=["Sb"])
                if mode == "B":
                    S.op("act", lambda e: e.activation(go[:], ps[:, 1, :], AF.Square, accum_out=gst[:, 0:1]),
                         reads=[("ps", 1)], writes=["go", "gst0"])
                    P.rstd(gst[:, 0:1], gst[:, 1:2], 1.0 / RDV, rc[:, 16 + h:17 + h], "gst0", "gst1")
                    S.op("dve", lambda e, c=c: e.scalar_tensor_tensor(go[:], ps[:, 1, :], gst[:, 1:2], sg[:, c, :],
                                                                      ALU.mult, ALU.mult),
                         reads=[("ps", 1), "gst1", ("sg", c)], writes=["go"])
                    P.transpose_to(go, 4, lambda k0, n, h=h, c=c: goT[:, h * 4 + k0:h * 4 + k0 + n,
                                                                        c * 128:(c + 1) * 128],
                                   ps_bf, ["go"], [("goT", c)], evac="act")
        if mode == "B":
            w_out_v = w_out.rearrange("(fc p) n -> p fc n", p=128)
            xin_r = x_in.rearrange("(t c p) d -> t p c d", p=128, c=4)
            xout_r = x_out.rearrange("(t c p) d -> t p c d", p=128, c=4)
            for dg in range(4):
                S.dma("sp", xr[:], xin_r[t][:, :, dg * 512:(dg + 1) * 512], writes=["xt"])
                for fb in range(4):
                    wv, wtok = P.load_w([(0, 512, w_out_v[:, fb * 8:(fb + 1) * 8, dg * 512:(dg + 1) * 512])], 8, 512)
                    for jx in range(8):
                        fc = fb * 8 + jx
                        for c in range(4):
                            S.op("pe", lambda e, fc=fc, jx=jx, c=c, wv=wv: e.matmul(
                                ps[:, c, :], goT[:, fc, c * 128:(c + 1) * 128], wv[:, jx, :],
                                start=(fc == 0), stop=(fc == 31)),
                                reads=[wtok, ("goT", c)], writes=[("ps", c)])
                for c in range(4):
                    S.op("dve", lambda e, c=c: e.tensor_tensor(xr[:, c, :], xr[:, c, :], ps[:, c, :], ALU.add),
                         reads=[("ps", c), "xt"], writes=["xt"])
                S.dma("sp", xout_r[t][:, :, dg * 512:(dg + 1) * 512], xr[:], reads=["xt"], writes=[("xout", t, dg)])
    if mode == "A":
        for h in range(RH):
            S.dma("sp", s_out[h].rearrange("(a p) e -> p a e", p=128), St[:, h], reads=["St"], writes=[("sout", h)])
    P.close()


def load_resident(P, dst, pieces_fn, a, nblk, bdim):
    S = P.S
    for b in range(nblk):
        wv, wtok = P.load_w(pieces_fn(b), a, bdim, cast_engs=("act",))
        S.op("pool", lambda e, b=b, wv=wv: e.tensor_copy(dst[:, :, b * bdim:(b + 1) * bdim], wv),
             reads=[wtok], writes=["resident"])


def ones_ssq(P, ps_acc, src_f, sq, ones_f, first, last, rtok, sqtok, acctok):
    S = P.S
    S.op("act", lambda e: e.activation(sq, src_f, AF.Square), reads=[rtok], writes=[sqtok])
    S.op("pe", lambda e: e.matmul(ps_acc, ones_f[:], sq, start=first, stop=last),
         reads=[sqtok, "ones_f"], writes=[acctok])


MH, MNOPE, MROPE, MV, KVR, QR = 16, 128, 64, 128, 512, 1536
SCALE = float((MNOPE + MROPE) ** -0.5)
SEQ = 8192


def rope_small(P, y, outb, cosv, sinv, nh, tmp_a, tmp_b, rd, wr):
    S = P.S
    cb = cosv.unsqueeze(1).broadcast_to([128, nh, 32])
    sb_ = sinv.unsqueeze(1).broadcast_to([128, nh, 32])
    x1, x2 = y[:, :, 0:32], y[:, :, 32:64]
    S.op("dve", lambda e: e.tensor_tensor(tmp_a, x1, cb, ALU.mult), rd, ["rs_a"])
    S.op("pool", lambda e: e.tensor_tensor(tmp_b, x2, sb_, ALU.mult), rd, ["rs_b"])
    S.op("dve", lambda e: e.tensor_tensor(outb[:, :, 0:32], tmp_a, tmp_b, ALU.subtract), ["rs_a", "rs_b"], wr)
    S.op("dve", lambda e: e.tensor_tensor(tmp_a, x2, cb, ALU.mult), rd, ["rs_a"])
    S.op("pool", lambda e: e.tensor_tensor(tmp_b, x1, sb_, ALU.mult), rd, ["rs_b"])
    S.op("dve", lambda e: e.tensor_tensor(outb[:, :, 32:64], tmp_a, tmp_b, ALU.add), ["rs_a", "rs_b"], wr)


def kv_phase(nc, S, cst, x_in, g_row, w_kvd, g_ckv_col, w_kvu, g_kn_col, g_kpe_row, pos_pc,
             KnT, KpeT, V, ntiles=TOK // T):
    P = Phase(nc, S, cst)
    P.setup_stream(4096)
    P.setup_norm(g_row)
    hT = P.sb("hT", [128, KC, T], BF16)
    wkd = P.sb("wkd", [128, KC, 576], BF16)
    wku = P.sb("wku", [128, 4, 4096], BF16)
    ones_f = P.sb("ones_f", [128, 128], F32)
    gcol = P.sb("gcol", [128, 4], F32)
    gkn = P.sb("gkn", [128, 1], F32)
    gpe = P.sb("gpe", [128, 64], F32)
    ckv_f = P.sb("ckv_f", [128, 4, T], F32)
    sq = [P.sb("sq%d" % i, [128, T], F32) for i in range(2)]
    rbc = P.sb("rbc", [128, T], F32)
    ckvT = P.sb("ckvT", [128, 4, T], BF16)
    kn_f = P.sb("kn_f", [128, T], F32)
    knb = P.sb("knb", [128, T], BF16)
    v_b = P.sb("v_b", [128, 2048], BF16)
    kpe_f = P.sb("kpe_f", [128, 64], F32)
    kpe_y = P.sb("kpe_y", [128, 1, 64], F32)
    kpe2 = P.sb("kpe2", [128, 2, 64], BF16)
    kpeT_sb = P.sb("kpeT_sb", [128, T], BF16)
    pst = P.sb("pst", [128, 2], F32)
    ta = P.sb("ta", [128, 1, 32], F32)
    tb_ = P.sb("tb_", [128, 1, 32], F32)
    ps = P.psum("ps", [128, 6, 512], F32)
    ps_bf = P.psum("ps_bf", [128, 2, 1024], BF16)
    S.op("pool", lambda e: e.memset(ones_f[:], 1.0), [], ["ones_f"])
    S.dma("sp", gcol[:], g_ckv_col, writes=["gcol"])
    S.dma("sp", gkn[:], g_kn_col, writes=["gkn"])
    S.dma("sp", gpe[:], g_kpe_row.partition_broadcast(128), writes=["gpe"])
    cos, sin, ctok, stok = P.rope_tables(pos_pc, cst["invf_mla"], 32, TOK // 128, "m")
    wkd_v = w_kvd.rearrange("(kc p) n -> p kc n", p=128)
    wku_v = w_kvu.rearrange("(kc p) n -> p kc n", p=128)
    load_resident(P, wkd, lambda b: [(0, 192, wkd_v[:, :, b * 192:(b + 1) * 192])], KC, 3, 192)
    load_resident(P, wku, lambda b: [(0, 512, wku_v[:, :, b * 512:(b + 1) * 512])], 4, 8, 512)
    xin_v = x_in.rearrange("(t c p) d -> t c p d", p=128, c=4)
    wku_h = wku[:].rearrange("p k (h two d) -> p k h two d", two=2, d=128)
    for t in range(ntiles):
        for c in range(4):
            P.norm_hT(xin_v[t, c], hT, c, ps_bf)
        for oc in range(4):
            bank = oc % 2
            for kc in range(KC):
                S.op("pe", lambda e, kc=kc, oc=oc, bank=bank: e.matmul(
                    ps[:, bank, :], wkd[:, kc, oc * 128:(oc + 1) * 128], hT[:, kc, :],
                    start=(kc == 0), stop=(kc == KC - 1)),
                    reads=["resident"] + [("hT", c) for c in range(4)], writes=[("ps", bank)])
            S.op("act", lambda e, oc=oc, bank=bank: e.copy(ckv_f[:, oc, :], ps[:, bank, :]),
                 reads=[("ps", bank)], writes=[("ckv_f", oc)])
            ones_ssq(P, ps[:, 2, :], ckv_f[:, oc, :], sq[oc % 2][:], ones_f, oc == 0, oc == 3,
                     ("ckv_f", oc), ("sq", oc % 2), ("ps", 2))
        P.rstd(ps[:, 2, :], rbc[:], 1.0 / KVR, EPS, ("ps", 2), "rbc")
        for oc in range(4):
            S.op("dve", lambda e, oc=oc: e.scalar_tensor_tensor(ckvT[:, oc, :], ckv_f[:, oc, :], gcol[:, oc:oc + 1],
                                                                 rbc[:], ALU.mult, ALU.mult),
                 reads=[("ckv_f", oc), "gcol", "rbc"], writes=["ckvT"])
        for c in range(4):
            j = t * 4 + c
            for kc in range(KC):
                S.op("pe", lambda e, kc=kc, c=c: e.matmul(ps[:, 3, 0:64], hT[:, kc, c * 128:(c + 1) * 128],
                                                          wkd[:, kc, 512:576], start=(kc == 0), stop=(kc == KC - 1)),
                     reads=["resident", ("hT", c)], writes=[("ps", 3)])
            S.op("act", lambda e: e.copy(kpe_f[:], ps[:, 3, 0:64]), reads=[("ps", 3)], writes=["kpe_f"])
            S.op("act", lambda e: e.activation(kpe_y[:, 0, :], kpe_f[:], AF.Square, accum_out=pst[:, 0:1]),
                 reads=["kpe_f"], writes=["kpe_y", "pst0"])
            P.rstd(pst[:, 0:1], pst[:, 1:2], 1.0 / MROPE, EPS, "pst0", "pst1")
            S.op("dve", lambda e: e.scalar_tensor_tensor(kpe_y[:, 0, :], kpe_f[:], pst[:, 1:2], gpe[:],
                                                          ALU.mult, ALU.mult),
                 reads=["kpe_f", "pst1", "gpe"], writes=["kpe_y"])
            rope_small(P, kpe_y[:], kpe2[:, 0:1, :], cos[:, j, :], sin[:, j, :], 1, ta[:], tb_[:],
                       ["kpe_y", ctok, stok], ["kpe2"])
            S.op("pool", lambda e: e.tensor_copy(kpe2[:, 1, :], kpe2[:, 0, :]), reads=["kpe2"], writes=["kpe2"])
            P.transpose_to(kpe2[:].rearrange("p a d -> p (a d)"), 1,
                           lambda k0, n, c=c: kpeT_sb[:, c * 128:(c + 1) * 128].unsqueeze(1),
                           ps_bf, ["kpe2"], ["kpeT_sb"], evac="act")
        S.dma("sp", KpeT[:, t * T:(t + 1) * T], kpeT_sb[:], reads=["kpeT_sb"], writes=[("KpeT", t)])
        for h in range(MH):
            bank = h % 2
            for kc in range(4):
                S.op("pe", lambda e, kc=kc, h=h, bank=bank: e.matmul(
                    ps[:, bank, :], wku_h[:, kc, h, 0, :], ckvT[:, kc, :], start=(kc == 0), stop=(kc == 3)),
                    reads=["resident", "ckvT"], writes=[("ps", bank)])
            S.op("act", lambda e, bank=bank: e.copy(kn_f[:], ps[:, bank, :]), reads=[("ps", bank)], writes=["kn_f"])
            ones_ssq(P, ps[:, 2, :], kn_f[:], sq[h % 2][:], ones_f, True, True, "kn_f", ("sq", h % 2), ("ps", 2))
            P.rstd(ps[:, 2, :], rbc[:], 1.0 / MNOPE, EPS, ("ps", 2), "rbc")
            S.op("dve", lambda e: e.scalar_tensor_tensor(knb[:], kn_f[:], gkn[:, 0:1], rbc[:], ALU.mult, ALU.mult),
                 reads=["kn_f", "gkn", "rbc"], writes=["knb"])
            S.dma("sp", KnT[h][:, t * T:(t + 1) * T], knb[:], reads=["knb"], writes=[("KnT", h, t)])
        for c in range(4):
            for hg in range(4):
                bank = 4 + hg % 2
                for hh in range(4):
                    for kc in range(4):
                        S.op("pe", lambda e, kc=kc, c=c, hg=hg, hh=hh, bank=bank: e.matmul(
                            ps[:, bank, hh * 128:(hh + 1) * 128], ckvT[:, kc, c * 128:(c + 1) * 128],
                            wku_h[:, kc, hg * 4 + hh, 1, :], start=(kc == 0), stop=(kc == 3)),
                            reads=["resident", "ckvT"], writes=[("ps", bank)])
                S.op("act", lambda e, hg=hg, bank=bank: e.copy(v_b[:, hg * 512:(hg + 1) * 512], ps[:, bank, :]),
                     reads=[("ps", bank)], writes=["v_b"])
            S.dma("sp", V[t * T + c * 128:t * T + (c + 1) * 128, :], v_b[:], reads=["v_b"], writes=[("V", t, c)])
    P.close()


def mla_layer(nc, S, cst, x_in, x_out, g_row, w_dq, g_q_col, w_uq, g_qn_col, g_qpe_row, w_o, pos_pc,
              qpos_row, KnT_all, KpeT_all, V_all, ntiles=TOK // T):
    outer = ExitStack()
    Phase.uid[0] += 1
    osfx = "_o%d" % Phase.uid[0]
    qnT = outer.enter_context(nc.sbuf_tensor("qnT" + osfx, [128, MH, T], BF16))
    qpeT = outer.enter_context(nc.sbuf_tensor("qpeT" + osfx, [128, 8, T], BF16))
    aoT = outer.enter_context(nc.sbuf_tensor("aoT" + osfx, [128, MH, T], BF16))
    xin_v = x_in.rearrange("(t c p) d -> t c p d", p=128, c=4)
    w_dq_v = w_dq.rearrange("(kc p) n -> p kc n", p=128)
    w_uq_v = w_uq.rearrange("(kc p) n -> p kc n", p=128)
    w_o_v = w_o.rearrange("(fc p) n -> p fc n", p=128)
    for t in range(ntiles):
        P = Phase(nc, S, cst)
        P.setup_stream(3072)
        P.setup_norm(g_row)
        hT = P.sb("hT", [128, KC, T], BF16)
        ones_f = P.sb("ones_f", [128, 128], F32)
        gq = P.sb("gq", [128, 12], F32)
        gqn = P.sb("gqn", [128, 1], F32)
        gqpe = P.sb("gqpe", [128, 64], F32)
        cq_f = P.sb("cq_f", [128, 12, T], F32)
        sq = [P.sb("sq%d" % i, [128, T], F32) for i in range(2)]
        rbc = P.sb("rbc", [128, T], F32)
        cqT = P.sb("cqT", [128, 12, T], BF16)
        qn_f = P.sb("qn_f", [128, T], F32)
        wpe = P.sb("wpe", [128, 12, 1024], BF16)
        qpe_f = P.sb("qpe_f", [128, MH, 64], F32)
        qpe_q = P.sb("qpe_q", [128, MH, 64], F32)
        qpe_b = P.sb("qpe_b", [128, MH, 64], BF16)
        r16 = P.sb("r16", [128, 2, MH], F32)
        ta = P.sb("ta", [128, MH, 32], F32)
        tb_ = P.sb("tb_", [128, MH, 32], F32)
        ps = P.psum("ps", [128, 6, 512], F32)
        ps_bf = P.psum("ps_bf", [128, 2, 1024], BF16)
        S.op("pool", lambda e: e.memset(ones_f[:], 1.0), [], ["ones_f"])
        S.dma("sp", gq[:], g_q_col, writes=["gq"])
        S.dma("sp", gqn[:], g_qn_col, writes=["gqn"])
        S.dma("sp", gqpe[:], g_qpe_row.partition_broadcast(128), writes=["gqpe"])
        cos, sin, ctok, stok = P.rope_tables(pos_pc[:, t * 4:(t + 1) * 4], cst["invf_mla"], 32, 4, "m")
        load_resident(P, wpe, lambda b: [(i * 64, 64, w_uq_v[:, :, (b * 4 + i) * 192 + 128:(b * 4 + i) * 192 + 192])
                                          for i in range(4)], 12, 4, 256)
        for c in range(4):
            P.norm_hT(xin_v[t, c], hT, c, ps_bf)
        for b in range(6):
            for oc2 in range(2):
                oc = b * 2 + oc2
                bank = oc % 2
                for kh in range(2):
                    wv, wtok = P.load_w([(0, 128, w_dq_v[:, kh * 8:(kh + 1) * 8, oc * 128:(oc + 1) * 128])], 8, 128)
                    for k8 in range(8):
                        kc = kh * 8 + k8
                        S.op("pe", lambda e, kc=kc, k8=k8, bank=bank, wv=wv: e.matmul(
                            ps[:, bank, :], wv[:, k8, :], hT[:, kc, :], start=(kc == 0), stop=(kc == KC - 1)),
                            reads=[wtok] + [("hT", c) for c in range(4)], writes=[("ps", bank)])
                S.op("act", lambda e, oc=oc, bank=bank: e.copy(cq_f[:, oc, :], ps[:, bank, :]),
                     reads=[("ps", bank)], writes=[("cq_f", oc)])
                ones_ssq(P, ps[:, 2, :], cq_f[:, oc, :], sq[oc % 2][:], ones_f, oc == 0, oc == 11,
                         ("cq_f", oc), ("sq", oc % 2), ("ps", 2))
        P.rstd(ps[:, 2, :], rbc[:], 1.0 / QR, EPS, ("ps", 2), "rbc")
        for oc in range(12):
            S.op("dve", lambda e, oc=oc: e.scalar_tensor_tensor(cqT[:, oc, :], cq_f[:, oc, :], gq[:, oc:oc + 1],
                                                                 rbc[:], ALU.mult, ALU.mult),
                 reads=[("cq_f", oc), "gq", "rbc"], writes=["cqT"])
        for hp in range(MH // 2):
            wv, wtok = P.load_w([(i * 128, 128, w_uq_v[:, :, (hp * 2 + i) * 192:(hp * 2 + i) * 192 + 128])
                                 for i in range(2)], 12, 256)
            for i in range(2):
                h = hp * 2 + i
                bank = h % 2
                for kc in range(12):
                    S.op("pe", lambda e, kc=kc, i=i, bank=bank, wv=wv: e.matmul(
                        ps[:, bank, :], wv[:, kc, i * 128:(i + 1) * 128], cqT[:, kc, :],
                        start=(kc == 0), stop=(kc == 11)),
                        reads=[wtok, "cqT"], writes=[("ps", bank)])
                S.op("act", lambda e, bank=bank: e.copy(qn_f[:], ps[:, bank, :]), reads=[("ps", bank)], writes=["qn_f"])
                ones_ssq(P, ps[:, 3, :], qn_f[:], sq[h % 2][:], ones_f, True, True, "qn_f", ("sq", h % 2), ("ps", 3))
                P.rstd(ps[:, 3, :], rbc[:], 1.0 / MNOPE, EPS, ("ps", 3), "rbc")
                S.op("dve", lambda e, h=h: e.scalar_tensor_tensor(qnT[:, h, :], qn_f[:], gqn[:, 0:1], rbc[:],
                                                                   ALU.mult, ALU.mult),
                     reads=["qn_f", "gqn", "rbc"], writes=["qnT"])
        for c in range(4):
            for g in range(2):
                for kc in range(12):
                    S.op("pe", lambda e, kc=kc, c=c, g=g: e.matmul(
                        ps[:, 4 + g, :], cqT[:, kc, c * 128:(c + 1) * 128], wpe[:, kc, g * 512:(g + 1) * 512],
                        start=(kc == 0), stop=(kc == 11)),
                        reads=["resident", "cqT"], writes=[("ps", 4 + g)])
                S.op("act", lambda e, g=g: e.copy(qpe_f[:, g * 8:(g + 1) * 8, :].rearrange("p h d -> p (h d)"),
                                                  ps[:, 4 + g, :]),
                     reads=[("ps", 4 + g)], writes=["qpe_f"])
            S.op("act", lambda e: e.activation(qpe_q[:].rearrange("p h d -> p (h d)"),
                                               qpe_f[:].rearrange("p h d -> p (h d)"), AF.Square),
                 reads=["qpe_f"], writes=["qpe_q"])
            S.op("dve", lambda e: e.tensor_reduce(r16[:, 0, :], qpe_q[:], AX.X, ALU.add),
                 reads=["qpe_q"], writes=["r16a"])
            P.rstd(r16[:, 0, :], r16[:, 1, :], 1.0 / MROPE, EPS, "r16a", "r16b")
            S.op("dve", lambda e: e.tensor_tensor(qpe_q[:], qpe_f[:],
                                                  r16[:, 1, :].unsqueeze(2).broadcast_to([128, MH, 64]), ALU.mult),
                 reads=["qpe_f", "r16b"], writes=["qpe_q"])
            S.op("dve", lambda e: e.tensor_tensor(qpe_q[:], qpe_q[:],
                                                  gqpe[:].unsqueeze(1).broadcast_to([128, MH, 64]), ALU.mult),
                 reads=["qpe_q", "gqpe"], writes=["qpe_q"])
            rope_small(P, qpe_q[:], qpe_b[:], cos[:, c, :], sin[:, c, :], MH, ta[:], tb_[:],
                       ["qpe_q", ctok, stok], ["qpe_b"])
            P.transpose_to(qpe_b[:].rearrange("p h d -> p (h d)"), 8,
                           lambda k0, n, c=c: qpeT[:, k0:k0 + n, c * 128:(c + 1) * 128],
                           ps_bf, ["qpe_b"], ["qpeT"], evac="act")
        P.close()

        P = Phase(nc, S, cst)
        kpe = P.sb("kpe", [128, SEQ], BF16)
        kbuf = [P.sb("kbuf%d" % i, [128, 2048], BF16) for i in range(2)]
        vbuf = [P.sb("vbuf%d" % i, [128, 16, 128], BF16) for i in range(2)]
        ebuf = [P.sb("ebuf%d" % i, [128, T], BF16) for i in range(2)]
        pbuf = [P.sb("pbuf%d" % i, [128, T], BF16) for i in range(2)]
        qpos = P.sb("qpos", [128, T], F32)
        kposc = P.sb("kposc", [128, 64], F32)
        ones_b = P.sb("ones_b", [128, 128], BF16)
        rec = P.sb("rec", [128, T], F32)
        ps = P.psum("ps", [128, 6, 512], F32)
        S.op("pool", lambda e: e.memset(ones_b[:], 1.0), [], ["ones_b"])
        S.dma("sp", qpos[:], qpos_row[t * T:(t + 1) * T].partition_broadcast(128), writes=["qpos"])
        S.dma("sp", kposc[:], cst["kposc"], writes=["kposc"])
        for q4 in range(4):
            S.dma("sp", kpe[:, q4 * 2048:(q4 + 1) * 2048], KpeT_all[:, q4 * 2048:(q4 + 1) * 2048], writes=["kpe"])
        blk = 0
        for h in range(MH):
            p0 = 64 * (h % 2)
            for g in range(4):
                bb = (h * 4 + g) % 2
                S.dma("sp", kbuf[bb][:], KnT_all[h][:, g * 2048:(g + 1) * 2048], writes=[("kbuf", bb)])
                S.dma("sp", vbuf[bb][:], V_all[g * 2048:(g + 1) * 2048, h * 128:(h + 1) * 128].rearrange(
                    "(i p) d -> p i d", p=128), writes=[("vbuf", bb)])
                for i in range(16):
                    kc = g * 16 + i
                    sb_ = blk % 2
                    blk += 1
                    S.op("pe", lambda e, bb=bb, i=i, h=h, sb_=sb_: e.matmul(
                        ps[:, sb_, :], kbuf[bb][:, i * 128:(i + 1) * 128], qnT[:, h, :], start=True, stop=False),
                        reads=[("kbuf", bb), "qnT"], writes=[("ps", sb_)])
                    S.op("pe", lambda e, kc=kc, h=h, p0=p0, sb_=sb_: e.matmul(
                        ps[:, sb_, :], kpe[p0:p0 + 64, kc * 128:(kc + 1) * 128], qpeT[p0:p0 + 64, h // 2, :],
                        start=False, stop=True),
                        reads=["kpe", "qpeT"], writes=[("ps", sb_)])
                    S.op("act", lambda e, sb_=sb_: e.activation(ebuf[sb_][:], ps[:, sb_, :], AF.Exp, scale=SCALE),
                         reads=[("ps", sb_)], writes=[("ebuf", sb_)])
                    S.op("dve", lambda e, kc=kc, sb_=sb_: e.scalar_tensor_tensor(
                        pbuf[sb_][:], qpos[:], kposc[:, kc:kc + 1], ebuf[sb_][:], ALU.is_ge, ALU.mult),
                        reads=[("ebuf", sb_), "qpos", "kposc"], writes=[("pbuf", sb_)])
                    S.op("pe", lambda e, bb=bb, i=i, kc=kc, sb_=sb_: e.matmul(
                        ps[:, 4, :], vbuf[bb][:, i, :], pbuf[sb_][:], start=(kc == 0), stop=(kc == 63)),
                        reads=[("vbuf", bb), ("pbuf", sb_)], writes=[("ps", 4)])
                    S.op("pe", lambda e, kc=kc, sb_=sb_: e.matmul(
                        ps[:, 5, :], ones_b[:], pbuf[sb_][:], start=(kc == 0), stop=(kc == 63)),
                        reads=["ones_b", ("pbuf", sb_)], writes=[("ps", 5)])
            S.op("dve", lambda e: e.reciprocal(rec[:], ps[:, 5, :]), reads=[("ps", 5)], writes=["rec"])
            S.op("dve", lambda e, h=h: e.tensor_tensor(aoT[:, h, :], ps[:, 4, :], rec[:], ALU.mult),
                 reads=[("ps", 4), "rec"], writes=["aoT"])
        P.close()

        P = Phase(nc, S, cst)
        P.setup_stream(4096)
        xr = P.sb("xr", [128, 4, 512], F32)
        ps = P.psum("ps", [128, 6, 512], F32)
        xin_r = x_in.rearrange("(t c p) d -> t p c d", p=128, c=4)
        xout_r = x_out.rearrange("(t c p) d -> t p c d", p=128, c=4)
        for dg in range(4):
            S.dma("sp", xr[:], xin_r[t][:, :, dg * 512:(dg + 1) * 512], writes=["xr"])
            for fb in range(2):
                wv, wtok = P.load_w([(0, 512, w_o_v[:, fb * 8:(fb + 1) * 8, dg * 512:(dg + 1) * 512])], 8, 512)
                for jx in range(8):
                    fc = fb * 8 + jx
                    for c in range(4):
                        S.op("pe", lambda e, fc=fc, jx=jx, c=c, wv=wv: e.matmul(
                            ps[:, c, :], aoT[:, fc, c * 128:(c + 1) * 128], wv[:, jx, :],
                            start=(fc == 0), stop=(fc == MH - 1)),
                            reads=[wtok, "aoT"], writes=[("ps", c)])
            for c in range(4):
                S.op("dve", lambda e, c=c: e.tensor_tensor(xr[:, c, :], xr[:, c, :], ps[:, c, :], ALU.add),
                     reads=[("ps", c), "xr"], writes=["xr"])
            S.dma("sp", xout_r[t][:, :, dg * 512:(dg + 1) * 512], xr[:], reads=["xr"], writes=[("xout", t, dg)])
        P.close()
    outer.close()


def host_consts():
    rc, cd, mask, lg = ret_consts()
    invf_ret = (1.0 / (10000.0 ** (np.arange(0, 256, 2, dtype=np.float32) / np.float32(256)))).astype(np.float32)
    invf_mla = (1.0 / (10000.0 ** (np.arange(0, 64, 2, dtype=np.float32) / np.float32(64)))).astype(np.float32)
    kposc = (np.arange(64, dtype=np.float32)[None, :] * 128 + np.arange(128, dtype=np.float32)[:, None])
    return {
        "ident": np.eye(128, dtype=np.float32),
        "rc": rc,
        "mask": mask,
        "invf_ret": np.ascontiguousarray(np.broadcast_to(invf_ret[None, :], (128, 128))),
        "invf_mla": np.ascontiguousarray(np.broadcast_to(invf_mla[None, :], (128, 32))),
        "kposc": np.ascontiguousarray(kposc.astype(np.float32)),
    }, lg


class Prog:
    def __init__(self):
        self.nc = bass.Bass("TRN2", target_bir_lowering=False)
        self.es = ExitStack()
        self.S = Sched(self.nc, self.es)
        self.ins = {}

    def inp(self, name, arr_example):
        dt = {"float32": F32, "int32": I32, "bfloat16": BF16}[str(arr_example.dtype)]
        ap = self.nc.dram_tensor(name, list(arr_example.shape), dt, kind="ExternalInput").ap()
        self.ins[name] = ap
        return ap

    def out(self, name, shape, dt):
        return self.nc.dram_tensor(name, list(shape), dt, kind="ExternalOutput").ap()

    def consts(self, cst_np):
        return {k: self.inp("c_" + k, v) for k, v in cst_np.items()}

    def run(self, in_maps):
        self.es.close()
        res = run_bass_kernel_spmd(self.nc, in_maps, core_ids=list(range(NCORE)))
        return res.results


def _cin(cst_np):
    return {"c_" + k: v for k, v in cst_np.items()}


def coef_table(lg):
    out = []
    for c in range(NCORE):
        r = c % 4
        t = np.zeros((4, RH), np.float64)
        for rp in range(r):
            t[rp] = np.exp(lg * (2048.0 * (r - rp - 1)))
        out.append(t.reshape(-1).astype(np.float32))
    return out


def launch_ret_a(cst_np, x_sh, pos_pc, g, w_in):
    pr = Prog()
    cst = pr.consts(cst_np)
    x = pr.inp("x", x_sh[0]); pos = pr.inp("pos", pos_pc[0]); ga = pr.inp("g", g); wi = pr.inp("w_in", w_in)
    so = pr.out("s_out", [RH, RDK, RDV], F32)
    ret_phase(pr.nc, pr.S, cst, "A", x, None, ga, wi, None, pos, None, None, so, 4)
    maps = [dict(_cin(cst_np), x=x_sh[c], pos=pos_pc[c], g=g, w_in=w_in) for c in range(NCORE)]
    res = pr.run(maps)
    return [r["s_out"] for r in res]


def gather_states(s_loc):
    return [np.ascontiguousarray(np.stack([s_loc[(c // 4) * 4 + r] for r in range(4)])) for c in range(NCORE)]


def launch_ret_b_mlp(cst_np, coefs, x_sh, pos_pc, g, w_in, w_out, s_all, g2, w1, w2, nxt):
    pr = Prog()
    cst = pr.consts(cst_np)
    x = pr.inp("x", x_sh[0]); pos = pr.inp("pos", pos_pc[0]); ga = pr.inp("g", g); wi = pr.inp("w_in", w_in)
    wo = pr.inp("w_out", w_out); sa = pr.inp("s_all", s_all[0]); cf = pr.inp("coef", coefs[0])
    g2a = pr.inp("g2", g2); w1a = pr.inp("w1", w1); w2a = pr.inp("w2", w2)
    xo = pr.out("xo", [TOK, D], F32)
    ret_phase(pr.nc, pr.S, cst, "B", x, xo, ga, wi, wo, pos, sa, cf, None, 4)
    mlp_phase(pr.nc, pr.S, cst, xo, xo, g2a, w1a, w2a)
    extra = {}
    outs = ["xo"]
    if nxt[0] == "ret_a":
        g3 = pr.inp("g3", nxt[1]); wi3 = pr.inp("w_in3", nxt[2])
        so = pr.out("s_out", [RH, RDK, RDV], F32)
        ret_phase(pr.nc, pr.S, cst, "A", xo, None, g3, wi3, None, pos, None, None, so, 4)
        extra = {"g3": nxt[1], "w_in3": nxt[2]}
        outs.append("s_out")
    else:
        kv = nxt[1]
        aps = {k: pr.inp("kv_" + k, v) for k, v in kv.items()}
        KnT = pr.out("KnT", [MH, 128, TOK], BF16)
        KpeT = pr.out("KpeT", [128, TOK], BF16)
        V = pr.out("V", [TOK, MH * MV], BF16)
        kv_phase(pr.nc, pr.S, cst, xo, aps["g"], aps["w_kvd"], aps["g_ckv_col"], aps["w_kvu"], aps["g_kn_col"],
                 aps["g_kpe_row"], pos, KnT, KpeT, V)
        extra = {"kv_" + k: v for k, v in kv.items()}
        outs += ["KnT", "KpeT", "V"]
    maps = [dict(_cin(cst_np), x=x_sh[c], pos=pos_pc[c], g=g, w_in=w_in, w_out=w_out, s_all=s_all[c],
                 coef=coefs[c], g2=g2, w1=w1, w2=w2, **extra) for c in range(NCORE)]
    res = pr.run(maps)
    return {k: [r[k] for r in res] for k in outs}


def launch_mla(cst_np, x_sh, pos_pc, qpos, kn_all, kpe_all, v_all, layers):
    pr = Prog()
    cst = pr.consts(cst_np)
    x = pr.inp("x", x_sh[0]); pos = pr.inp("pos", pos_pc[0]); qp = pr.inp("qpos", qpos[0])
    kn = pr.inp("kn", kn_all[0]); kp = pr.inp("kp", kpe_all[0]); va = pr.inp("va", v_all[0])
    xo = pr.out("xo", [TOK, D], F32)
    shared = {}
    src = x
    for li, L in enumerate(layers):
        a = {k: pr.inp("L%d_%s" % (li, k), v) for k, v in L.items()}
        shared.update({"L%d_%s" % (li, k): v for k, v in L.items()})
        mla_layer(pr.nc, pr.S, cst, src, xo, a["g"], a["w_dq"], a["g_q_col"], a["w_uq"], a["g_qn_col"],
                  a["g_qpe_row"], a["w_o"], pos, qp, kn, kp, va)
        mlp_phase(pr.nc, pr.S, cst, xo, xo, a["g2"], a["w1"], a["w2"])
        src = xo
    maps = [dict(_cin(cst_np), x=x_sh[c], pos=pos_pc[c], qpos=qpos[c], kn=kn_all[c], kp=kpe_all[c], va=v_all[c],
                 **shared) for c in range(NCORE)]
    res = pr.run(maps)
    return [r["xo"] for r in res]


def col_layout(v):
    v = np.asarray(v, np.float32)
    return np.ascontiguousarray(v.reshape(-1, 128).T)


def kernel(x, positions, norm_mix, norm_mlp, ret_w_in, ret_w_out, kv_norm_in,
           mla_w_kv_down, mla_kv_norm, mla_w_kv_up, mla_k_nope_norm, mla_k_pe_norm,
           mla_w_dq, mla_q_norm, mla_w_uq, mla_q_nope_norm, mla_q_pe_norm, mla_w_o,
           mlp_w1, mlp_w2):
    f = lambda a: np.ascontiguousarray(np.asarray(a, dtype=np.float32))
    cst_np, lg = host_consts()
    coefs = coef_table(lg)
    x = f(x)
    positions = np.asarray(positions, dtype=np.int32)
    x_sh = [np.ascontiguousarray(x[c // 4, (c % 4) * TOK:(c % 4 + 1) * TOK]) for c in range(NCORE)]
    pos_pc = [np.ascontiguousarray(positions[c // 4, (c % 4) * TOK:(c % 4 + 1) * TOK].reshape(16, 128).T)
              for c in range(NCORE)]
    qpos = [np.arange((c % 4) * TOK, (c % 4 + 1) * TOK, dtype=np.float32) for c in range(NCORE)]
    norm_mix, norm_mlp = f(norm_mix), f(norm_mlp)
    ret_w_in, ret_w_out, mlp_w1, mlp_w2 = f(ret_w_in), f(ret_w_out), f(mlp_w1), f(mlp_w2)

    s_loc = launch_ret_a(cst_np, x_sh, pos_pc, norm_mix[0], ret_w_in[0])
    r = launch_ret_b_mlp(cst_np, coefs, x_sh, pos_pc, norm_mix[0], ret_w_in[0], ret_w_out[0], gather_states(s_loc),
                         norm_mlp[0], mlp_w1[0], mlp_w2[0], ("ret_a", norm_mix[1], ret_w_in[1]))
    x_sh = r["xo"]
    kv = {"g": f(kv_norm_in), "w_kvd": f(mla_w_kv_down), "g_ckv_col": col_layout(mla_kv_norm),
          "w_kvu": f(mla_w_kv_up), "g_kn_col": col_layout(mla_k_nope_norm), "g_kpe_row": f(mla_k_pe_norm)}
    r = launch_ret_b_mlp(cst_np, coefs, x_sh, pos_pc, norm_mix[1], ret_w_in[1], ret_w_out[1],
                         gather_states(r["s_out"]), norm_mlp[1], mlp_w1[1], mlp_w2[1], ("kv", kv))
    x_sh = r["xo"]
    kn_all, kpe_all, v_all = [], [], []
    for c in range(NCORE):
        b = c // 4
        kn_all.append(np.ascontiguousarray(np.concatenate([r["KnT"][b * 4 + q] for q in range(4)], axis=2)))
        kpe_all.append(np.ascontiguousarray(np.concatenate([r["KpeT"][b * 4 + q] for q in range(4)], axis=1)))
        v_all.append(np.ascontiguousarray(np.concatenate([r["V"][b * 4 + q] for q in range(4)], axis=0)))
    layers = []
    for j in range(2):
        layers.append({"g": norm_mix[2 + j], "w_dq": f(mla_w_dq[j]), "g_q_col": col_layout(mla_q_norm[j]),
                       "w_uq": f(mla_w_uq[j]), "g_qn_col": col_layout(mla_q_nope_norm[j]),
                       "g_qpe_row": f(mla_q_pe_norm[j]), "w_o": f(mla_w_o[j]),
                       "g2": norm_mlp[2 + j], "w1": mlp_w1[2 + j], "w2": mlp_w2[2 + j]})
    xo = launch_mla(cst_np, x_sh, pos_pc, qpos, kn_all, kpe_all, v_all, layers)
    out = np.empty((2, SEQ, D), np.float32)
    for c in range(NCORE):
        out[c // 4, (c % 4) * TOK:(c % 4 + 1) * TOK] = xo[c]
    return out
```

```python
import math
from contextlib import ExitStack

import numpy as np
import concourse.bass as bass
import concourse.mybir as mybir
from concourse.bass_utils import run_bass_kernel_spmd

F32 = mybir.dt.float32
BF16 = mybir.dt.bfloat16
I32 = mybir.dt.int32
AF = mybir.ActivationFunctionType
ALU = mybir.AluOpType
AX = mybir.AxisListType

D = 2048
DFF = 8192
NCORE = 8
TOK = 2048
T = 512
EPS = 1e-6

COMPUTE = ("pe", "act", "dve", "pool")
DMAQ = ("sp", "act", "pool")
NDS = 8


class Op:
    __slots__ = ("eng", "fn", "deps", "signal", "sigval", "is_dma", "dma_n", "idx")

    def __init__(self, eng, fn, is_dma):
        self.eng = eng
        self.fn = fn
        self.deps = []
        self.signal = False
        self.sigval = 0
        self.is_dma = is_dma
        self.dma_n = -1
        self.idx = -1


class Sched:
    def __init__(self, nc, es):
        self.nc = nc
        self.csem = {e: es.enter_context(nc.semaphore("c_" + e)) for e in COMPUTE}
        self.ccount = {e: 0 for e in COMPUTE}
        self.dsem = {q: [es.enter_context(nc.semaphore("d_%s%d" % (q, i))) for i in range(NDS)]
                     for q in DMAQ}
        self.dcount = {q: 0 for q in DMAQ}
        self.known = {e: {} for e in ("pe", "act", "dve", "pool", "sp")}
        self.reset_phase()

    def reset_phase(self):
        self.ops = {e: [] for e in ("pe", "act", "dve", "pool", "sp")}
        self.lastw = {}
        self.readers = {}

    def op(self, eng, fn, reads=(), writes=(), dma=False):
        o = Op(eng, fn, dma)
        if dma:
            o.dma_n = self.dcount[eng]
            self.dcount[eng] += 1
        deps = {}
        for t in reads:
            w = self.lastw.get(t)
            if w is not None:
                deps[id(w)] = w
        for t in writes:
            w = self.lastw.get(t)
            if w is not None:
                deps[id(w)] = w
            for r in self.readers.get(t, ()):
                deps[id(r)] = r
        for d in deps.values():
            if d is o:
                continue
            if d.eng == "pe" and eng == "pe" and not d.is_dma and not dma:
                continue
            o.deps.append(d)
            if not d.is_dma:
                d.signal = True
        for t in reads:
            lst = self.readers.setdefault(t, [])
            if not dma:
                lst[:] = [r for r in lst if r.is_dma or r.eng != eng]
            lst.append(o)
        for t in writes:
            self.lastw[t] = o
            self.readers[t] = []
        o.idx = len(self.ops[eng])
        self.ops[eng].append(o)
        return o

    def dma(self, q, out, in_, reads=(), writes=()):
        return self.op(q, lambda e: e.dma_start(out=out, in_=in_), reads, writes, dma=True)

    def _dma_wait(self, d):
        return (self.dsem[d.eng][d.dma_n % NDS], 16 * (d.dma_n // NDS + 1))

    def emit(self, name=None):
        nc = self.nc
        for e in COMPUTE:
            cops = [o for o in self.ops[e] if not o.is_dma]
            if cops:
                cops[-1].signal = True
        for e in COMPUTE:
            for o in self.ops[e]:
                if not o.is_dma and o.signal:
                    self.ccount[e] += 1
                    o.sigval = self.ccount[e]
        final_c = dict(self.ccount)
        final_d = dict(self.dcount)

        def run(ename, eng):
            known = self.known[ename]

            def wait(sem, val, key):
                if known.get(key, 0) >= val:
                    return
                known[key] = val
                eng.wait_ge(sem, val)

            for o in self.ops[ename]:
                for d in o.deps:
                    if d.is_dma:
                        sem, val = self._dma_wait(d)
                        wait(sem, val, ("d", d.eng, d.dma_n % NDS))
                    else:
                        wait(self.csem[d.eng], d.sigval, ("c", d.eng))
                if o.is_dma:
                    slot = o.dma_n % NDS
                    if o.dma_n >= NDS:
                        wait(self.dsem[ename][slot], 16 * (o.dma_n // NDS), ("d", ename, slot))
                    o.fn(eng).then_inc(self.dsem[ename][slot], 16)
                else:
                    ins = o.fn(eng)
                    if o.signal:
                        ins.then_inc(self.csem[ename], 1)
            for e in COMPUTE:
                if final_c[e] > 0:
                    wait(self.csem[e], final_c[e], ("c", e))
            for q in DMAQ:
                n = final_d[q]
                for slot in range(NDS):
                    if n > slot:
                        cnt = (n - 1 - slot) // NDS + 1
                        wait(self.dsem[q][slot], 16 * cnt, ("d", q, slot))

        with nc.Block() as block:
            @block.tensor
            def _(eng):
                run("pe", eng)

            @block.scalar
            def _(eng):
                run("act", eng)

            @block.vector
            def _(eng):
                run("dve", eng)

            @block.gpsimd
            def _(eng):
                run("pool", eng)

            @block.sync
            def _(eng):
                run("sp", eng)
        self.reset_phase()


def cast_op(S, eng, out, in_, reads, writes):
    if eng == "act":
        return S.op("act", lambda e: e.copy(out, in_), reads, writes)
    return S.op(eng, lambda e: e.tensor_copy(out, in_), reads, writes)


KC = D // 128
FC = DFF // 128
TWO_PI = 2.0 * math.pi
CW1 = 6.28125
CW2 = TWO_PI - CW1


class Phase:
    uid = [0]

    def __init__(self, nc, S, cst):
        self.nc, self.S, self.cst = nc, S, cst
        self.es = ExitStack()
        Phase.uid[0] += 1
        self.sfx = "_p%d" % Phase.uid[0]
        self.wcount = 0
        self.ident = self.sb("ident", [128, 128], BF16)
        identf = self.sb("identf", [128, 128], F32)
        S.dma("sp", identf[:], cst["ident"], writes=["identf"])
        cast_op(S, "dve", self.ident[:], identf[:], ["identf"], ["ident"])

    def sb(self, name, shape, dt):
        return self.es.enter_context(self.nc.sbuf_tensor(name + self.sfx, shape, dt))

    def psum(self, name, shape, dt):
        return self.es.enter_context(self.nc.psum_tensor(name + self.sfx, shape, dt))

    def close(self):
        self.S.emit()
        self.es.close()

    def setup_stream(self, n_elem=4096):
        self.wst = [self.sb("wst%d" % i, [128, n_elem], F32) for i in range(2)]
        self.wbf = [self.sb("wbf%d" % i, [128, n_elem], BF16) for i in range(2)]

    def load_w(self, pieces, a, bdim, cast_engs=("act", "pool")):
        S = self.S
        i = self.wcount
        self.wcount += 1
        b = i % 2
        stv = self.wst[b][:, 0:a * bdim].rearrange("p (a b) -> p a b", a=a)
        bfv = self.wbf[b][:, 0:a * bdim].rearrange("p (a b) -> p a b", a=a)
        for (off, n, src) in pieces:
            S.dma("sp", stv[:, :, off:off + n], src, reads=[("wst", b)], writes=[("wst", b)])
        ce = cast_engs[i % len(cast_engs)]
        cast_op(S, ce, bfv, stv, [("wst", b)], [("wbf", b)])
        return bfv, ("wbf", b)

    def setup_norm(self, g_row):
        self.xt = self.sb("xt", [128, D], F32)
        self.hn = self.sb("hn", [128, D], BF16)
        self.gbc = self.sb("gbc", [128, D], F32)
        self.nst = self.sb("nst", [128, 2], F32)
        self.S.dma("sp", self.gbc[:], g_row.partition_broadcast(128), writes=["gbc"])

    def norm_hT(self, x_chunk, hT, c, ps_bf, KCn=KC):
        S, xt, hn, gbc, st = self.S, self.xt, self.hn, self.gbc, self.nst
        S.dma("sp", xt[:], x_chunk, writes=["xt"])
        S.op("act", lambda e: e.activation(hn[:], xt[:], AF.Square, accum_out=st[:, 0:1]),
             reads=["xt"], writes=["hn", "nst0"])
        self.rstd(st[:, 0:1], st[:, 1:2], 1.0 / D, EPS, "nst0", "nst1")
        S.op("dve", lambda e: e.scalar_tensor_tensor(hn[:], xt[:], st[:, 1:2], gbc[:], ALU.mult, ALU.mult),
             reads=["xt", "nst1", "gbc"], writes=["hn"])
        self.transpose_to(hn, KCn, lambda k0, n: hT[:, k0:k0 + n, c * 128:(c + 1) * 128], ps_bf,
                          ["hn"], [("hT", c)])

    def rstd(self, ssq, out, scale, eps, rtok, wtok):
        S = self.S
        if isinstance(eps, float):
            S.op("dve", lambda e: e.tensor_scalar(out, ssq, scale, eps, ALU.mult, ALU.add),
                 reads=[rtok], writes=[wtok])
        else:
            S.op("dve", lambda e: e.scalar_tensor_tensor(out, ssq, scale, eps, ALU.mult, ALU.add),
                 reads=[rtok, "rc"], writes=[wtok])
        S.op("act", lambda e: e.sqrt(out, out), reads=[wtok], writes=[wtok])
        S.op("dve", lambda e: e.reciprocal(out, out), reads=[wtok], writes=[wtok])

    def transpose_to(self, src, nblk, dst_fn, ps_bf, reads, writes, evac="act"):
        S = self.S
        k0 = 0
        while k0 < nblk:
            n = min(8, nblk - k0)
            bank = self.tb = (getattr(self, "tb", 0) + 1) % 2
            for k in range(n):
                S.op("pe", lambda e, k=k, k0=k0, bank=bank: e.transpose(
                    ps_bf[:, bank, k * 128:(k + 1) * 128], src[:, (k0 + k) * 128:(k0 + k + 1) * 128],
                    self.ident[:]), reads=list(reads) + ["ident"], writes=[("psb", bank)])
            dst = dst_fn(k0, n)
            srcv = ps_bf[:, bank, 0:n * 128].rearrange("p (k n) -> p k n", k=n)
            if evac == "act":
                S.op("act", lambda e, dst=dst, srcv=srcv: e.copy(dst, srcv), reads=[("psb", bank)], writes=writes)
            else:
                S.op(evac, lambda e, dst=dst, srcv=srcv: e.tensor_copy(dst, srcv), reads=[("psb", bank)],
                     writes=writes)
            k0 += n

    def rope_tables(self, pos_d, invf_d, nfreq, nchunk, name):
        S = self.S
        gsz = max(1, min(nchunk, 512 // nfreq))
        W = gsz * nfreq
        posi = self.sb(name + "posi", [128, nchunk], I32)
        posf = self.sb(name + "posf", [128, nchunk], F32)
        invf = self.sb(name + "invf", [128, nfreq], F32)
        ang = self.sb(name + "ang", [128, gsz, nfreq], F32)
        tmp = self.sb(name + "tmp", [128, W], F32)
        ki = self.sb(name + "ki", [128, W], I32)
        kf = self.sb(name + "kf", [128, W], F32)
        cos = self.sb(name + "cos", [128, nchunk, nfreq], F32)
        sin = self.sb(name + "sin", [128, nchunk, nfreq], F32)
        tk = name + "rt"
        S.dma("sp", posi[:], pos_d, writes=[tk + "posi"])
        S.dma("sp", invf[:], invf_d, writes=[tk + "invf"])
        S.op("dve", lambda e: e.tensor_copy(posf[:], posi[:]), [tk + "posi"], [tk + "posf"])
        angf = ang[:].rearrange("p c f -> p (c f)")
        for g0 in range(0, nchunk, gsz):
            for c in range(gsz):
                S.op("dve", lambda e, c=c, g0=g0: e.tensor_scalar(ang[:, c, :], invf[:], posf[:, g0 + c:g0 + c + 1],
                                                                  None, ALU.mult),
                     [tk + "posf", tk + "invf"], [tk + "ang"])
            for (dst, shift, wt) in ((sin, 0.0, tk + "sin"), (cos, 0.25, tk + "cos")):
                dstf = dst[:, g0:g0 + gsz, :].rearrange("p c f -> p (c f)")
                S.op("dve", lambda e, shift=shift: e.tensor_scalar(tmp[:], angf, 1.0 / TWO_PI, shift,
                                                                   ALU.mult, ALU.add),
                     [tk + "ang"], [tk + "tmp"])
                S.op("dve", lambda e: e.tensor_copy(ki[:], tmp[:]), [tk + "tmp"], [tk + "ki"])
                S.op("dve", lambda e: e.tensor_copy(kf[:], ki[:]), [tk + "ki"], [tk + "kf"])
                S.op("dve", lambda e: e.scalar_tensor_tensor(tmp[:], kf[:], -CW1, angf, ALU.mult, ALU.add),
                     [tk + "kf", tk + "ang"], [tk + "tmp"])
                S.op("dve", lambda e: e.scalar_tensor_tensor(tmp[:], kf[:], -CW2, tmp[:], ALU.mult, ALU.add),
                     [tk + "kf", tk + "tmp"], [tk + "tmp"])
                S.op("dve", lambda e, shift=shift: e.tensor_scalar(tmp[:], tmp[:], shift * TWO_PI, math.pi,
                                                                   ALU.add, ALU.min),
                     [tk + "tmp"], [tk + "tmp"])
                S.op("dve", lambda e: e.tensor_scalar(tmp[:], tmp[:], -math.pi, None, ALU.max),
                     [tk + "tmp"], [tk + "tmp"])
                S.op("act", lambda e, dstf=dstf: e.activation(dstf, tmp[:], AF.Sin), [tk + "tmp"], [wt])
        return cos, sin, tk + "cos", tk + "sin"


def mlp_phase(nc, S, cst, x_in, x_out, g_row, w1, w2, ntiles=TOK // T):
    W1B = 256
    P = Phase(nc, S, cst)
    P.setup_stream(4096)
    P.setup_norm(g_row)
    xr = P.sb("xr", [128, 4, 512], F32)
    hT = P.sb("hT", [128, KC, T], BF16)
    hid = P.sb("hid", [128, FC, T], BF16)
    rl = [P.sb("rl%d" % i, [128, T], F32) for i in range(2)]
    ps = P.psum("ps", [128, 6, 512], F32)
    ps_bf = P.psum("ps_bf", [128, 2, 1024], BF16)
    w1v = w1.rearrange("(kc p) n -> p kc n", p=128)
    w2v = w2.rearrange("(fc p) n -> p fc n", p=128)
    xin_v = x_in.rearrange("(t c p) d -> t c p d", p=128, c=4)
    xin_r = x_in.rearrange("(t c p) d -> t p c d", p=128, c=4)
    xout_r = x_out.rearrange("(t c p) d -> t p c d", p=128, c=4)
    for t in range(ntiles):
        for c in range(4):
            P.norm_hT(xin_v[t, c], hT, c, ps_bf)
        for b in range(DFF // W1B):
            wv, wtok = P.load_w([(0, W1B, w1v[:, :, b * W1B:(b + 1) * W1B])], KC, W1B)
            for oc in range(W1B // 128):
                bank = (b * 2 + oc) % 2 + 4
                for kc in range(KC):
                    S.op("pe", lambda e, kc=kc, oc=oc, bank=bank, wv=wv: e.matmul(
                        ps[:, bank, :], wv[:, kc, oc * 128:(oc + 1) * 128], hT[:, kc, :],
                        start=(kc == 0), stop=(kc == KC - 1)),
                        reads=[wtok] + [("hT", c) for c in range(4)], writes=[("ps", bank)])
                fc = b * (W1B // 128) + oc
                rb = fc % 2
                S.op("act", lambda e, rb=rb, bank=bank: e.activation(rl[rb][:], ps[:, bank, :], AF.Relu),
                     reads=[("ps", bank)], writes=[("rl", rb)])
                S.op("dve", lambda e, fc=fc, rb=rb: e.tensor_tensor(hid[:, fc, :], rl[rb][:], rl[rb][:], ALU.mult),
                     reads=[("rl", rb)], writes=[("hid", fc)])
        for dg in range(4):
            S.dma("sp", xr[:], xin_r[t][:, :, dg * 512:(dg + 1) * 512], writes=["xr"])
            for fb in range(8):
                wv, wtok = P.load_w([(0, 512, w2v[:, fb * 8:(fb + 1) * 8, dg * 512:(dg + 1) * 512])], 8, 512)
                for j in range(8):
                    fc = fb * 8 + j
                    for c in range(4):
                        S.op("pe", lambda e, fc=fc, j=j, c=c, wv=wv: e.matmul(
                            ps[:, c, :], hid[:, fc, c * 128:(c + 1) * 128], wv[:, j, :],
                            start=(fc == 0), stop=(fc == FC - 1)),
                            reads=[wtok, ("hid", fc)], writes=[("ps", c)])
            for c in range(4):
                S.op("dve", lambda e, c=c: e.tensor_tensor(xr[:, c, :], xr[:, c, :], ps[:, c, :], ALU.add),
                     reads=[("ps", c), "xr"], writes=["xr"])
            S.dma("sp", xout_r[t][:, :, dg * 512:(dg + 1) * 512], xr[:], reads=["xr"], writes=[("xout", t, dg)])
    P.close()


RH, RDK, RDV = 8, 256, 512


def ret_consts():
    lg = np.log1p(-np.exp2(-5.0 - np.arange(RH, dtype=np.float64)))
    n = np.arange(128, dtype=np.float64)
    ginv = np.exp(-(n[:, None] + 1.0) * lg[None, :]) / 16.0
    zeta = np.exp((127.0 - n[:, None]) * lg[None, :]) / 16.0
    epsx = EPS / np.exp(2.0 * (n[:, None] + 1.0) * lg[None, :])
    rc = np.concatenate([ginv, zeta, epsx], axis=1).astype(np.float32)
    cd = [float(np.exp(128.0 * v)) for v in lg]
    mask = (n[None, :] >= n[:, None]).astype(np.float32)
    return rc, cd, mask, lg


def ret_phase(nc, S, cst, mode, x_in, x_out, g_row, w_in, w_out, pos, s_all, coef, s_out, nrank,
              ntiles=TOK // T):
    rc_np, cd, _, _ = ret_consts()
    P = Phase(nc, S, cst)
    P.setup_stream(4096)
    P.setup_norm(g_row)
    hT = P.sb("hT", [128, KC, T], BF16)
    St = P.sb("St", [128, RH, 2, 512], F32)
    Sb = P.sb("Sb", [128, 2, 512], BF16)
    rc = P.sb("rc", [128, 24], F32)
    mask = P.sb("mask", [128, 128], F32)
    qk_f = P.sb("qk_f", [128, 4, 512], F32)
    qk_b = P.sb("qk_b", [128, 4, 512], BF16)
    kz = P.sb("kz", [128, 256], BF16)
    qkT = P.sb("qkT", [128, 4, 128], BF16)
    v_b = P.sb("v_b", [128, 4, 512], BF16)
    ra = P.sb("ra", [128, 2, 128], F32)
    rb_ = P.sb("rb_", [128, 2, 128], F32)
    ps = P.psum("ps", [128, 6, 512], F32)
    ps_bf = P.psum("ps_bf", [128, 2, 1024], BF16)
    S.dma("sp", rc[:], cst["rc"], writes=["rc"])
    S.dma("sp", mask[:], cst["mask"], writes=["mask"])
    if mode == "B":
        goT = P.sb("goT", [128, 32, T], BF16)
        sg = P.sb("sg", [128, 4, 512], BF16)
        PT = P.sb("PT", [128, 128], BF16)
        go = P.sb("go", [128, 512], BF16)
        gst = P.sb("gst", [128, 2], F32)
        xr = P.xt[:].rearrange("p (c n) -> p c n", c=4)
        cf = P.sb("cf", [128, nrank * RH], F32)
        stmp = P.sb("stmp", [128, 2, 512], F32)
    cos, sin, ctok, stok = P.rope_tables(pos, cst["invf_ret"], 128, TOK // 128, "r")

    S.op("pool", lambda e: e.memset(St[:].rearrange("p h a b -> p (h a b)"), 0.0), [], ["St"])
    if mode == "B":
        S.dma("sp", cf[:], coef.partition_broadcast(128), writes=["cf"])
        for r in range(nrank):
            for h in range(RH):
                S.dma("sp", stmp[:], s_all[r, h].rearrange("(a p) e -> p a e", p=128), writes=["stmp"])
                S.op("dve", lambda e, r=r, h=h: e.scalar_tensor_tensor(
                    St[:, h].rearrange("p a b -> p (a b)"), stmp[:].rearrange("p a b -> p (a b)"),
                    cf[:, r * RH + h:r * RH + h + 1], St[:, h].rearrange("p a b -> p (a b)"), ALU.mult, ALU.add),
                    reads=["stmp", "cf", "St"], writes=["St"])

    w_in_v = w_in.rearrange("(kc p) n -> p kc n", p=128)
    xin_v = x_in.rearrange("(t c p) d -> t c p d", p=128, c=4)
    QO, KO, VO, GO = 0, RH * RDK, 2 * RH * RDK, 2 * RH * RDK + RH * RDV

    def proj256(col0, evac):
        wv, wtok = P.load_w([(0, 256, w_in_v[:, :, col0:col0 + 256])], KC, 256)
        for c in range(4):
            bank = 4 + (P.wcount * 4 + c) % 2
            for kc in range(KC):
                S.op("pe", lambda e, kc=kc, c=c, bank=bank, wv=wv: e.matmul(
                    ps[:, bank, 0:256], hT[:, kc, c * 128:(c + 1) * 128], wv[:, kc, :],
                    start=(kc == 0), stop=(kc == KC - 1)),
                    reads=[wtok, ("hT", c)], writes=[("ps", bank)])
            evac(c, ps[:, bank, 0:256], ("ps", bank))

    for t in range(ntiles):
        for c in range(4):
            P.norm_hT(xin_v[t, c], hT, c, ps_bf)
        for h in range(RH):
            if mode == "B":
                proj256(QO + h * RDK, lambda c, p, tk: S.op(
                    "act", lambda e: e.copy(qk_f[:, c, 0:256], p), reads=[tk], writes=[("qk_f", c)]))
            proj256(KO + h * RDK, lambda c, p, tk: S.op(
                "act", lambda e: e.copy(qk_f[:, c, 256:512], p), reads=[tk], writes=[("qk_f", c)]))
            for half in range(2):
                proj256(VO + h * RDV + half * 256, lambda c, p, tk, half=half: S.op(
                    "act", lambda e: e.copy(v_b[:, c, half * 256:(half + 1) * 256], p),
                    reads=[tk], writes=[("v_b", c)]))
            if mode == "B":
                for half in range(2):
                    proj256(GO + h * RDV + half * 256, lambda c, p, tk, half=half: S.op(
                        "act", lambda e: e.activation(sg[:, c, half * 256:(half + 1) * 256], p, AF.Silu),
                        reads=[tk], writes=[("sg", c)]))
            S.op("pool", lambda e, h=h: e.tensor_copy(Sb[:], St[:, h]), reads=["St"], writes=["Sb"])
            for c in range(4):
                j = t * 4 + c
                xv = qk_f[:, c, :].rearrange("p (a b f) -> p a b f", a=2, b=2)
                ov = qk_b[:, c, :].rearrange("p (a b f) -> p a b f", a=2, b=2)
                cb = cos[:, j, :].unsqueeze(1).broadcast_to([128, 2, 128])
                sbb = sin[:, j, :].unsqueeze(1).broadcast_to([128, 2, 128])
                rd = [("qk_f", c), ctok, stok]
                S.op("dve", lambda e, xv=xv, cb=cb: e.tensor_tensor(ra[:], xv[:, :, 0, :], cb, ALU.mult), rd, ["ra"])
                S.op("pool", lambda e, xv=xv, sbb=sbb: e.tensor_tensor(rb_[:], xv[:, :, 1, :], sbb, ALU.mult), rd, ["rb"])
                S.op("dve", lambda e, ov=ov: e.tensor_tensor(ov[:, :, 0, :], ra[:], rb_[:], ALU.subtract),
                     ["ra", "rb"], [("qk_b", c)])
                S.op("dve", lambda e, xv=xv, cb=cb: e.tensor_tensor(ra[:], xv[:, :, 1, :], cb, ALU.mult), rd, ["ra"])
                S.op("pool", lambda e, xv=xv, sbb=sbb: e.tensor_tensor(rb_[:], xv[:, :, 0, :], sbb, ALU.mult), rd, ["rb"])
                S.op("dve", lambda e, ov=ov: e.tensor_tensor(ov[:, :, 1, :], ra[:], rb_[:], ALU.add),
                     ["ra", "rb"], [("qk_b", c)])
                S.op("pool", lambda e, c=c, h=h: e.tensor_scalar(kz[:], qk_b[:, c, 256:512], rc[:, 8 + h:9 + h], None,
                                                                  ALU.mult), [("qk_b", c), "rc"], ["kz"])
                if mode == "B":
                    P.transpose_to(qk_b[:, c, :], 4, lambda k0, n: qkT[:, k0:k0 + n, :], ps_bf,
                                   [("qk_b", c)], ["qkT"], evac="act")
                    for dc in range(2):
                        S.op("pe", lambda e, dc=dc: e.matmul(ps[:, 0, 0:128], qkT[:, 2 + dc, :], qkT[:, dc, :],
                                                             start=(dc == 0), stop=(dc == 1)),
                             reads=["qkT"], writes=[("ps", 0)])
                    S.op("dve", lambda e, h=h: e.scalar_tensor_tensor(PT[:], ps[:, 0, 0:128], rc[:, h:h + 1], mask[:],
                                                                      ALU.mult, ALU.mult),
                         reads=[("ps", 0), "rc", "mask"], writes=["PT"])
                    for dc in range(2):
                        S.op("pe", lambda e, dc=dc: e.matmul(ps[:, 1, :], qkT[:, dc, :], Sb[:, dc, :],
                                                             start=(dc == 0), stop=False),
                             reads=["qkT", "Sb"], writes=[("ps", 1)])
                    S.op("pe", lambda e, c=c: e.matmul(ps[:, 1, :], PT[:], v_b[:, c, :], start=False, stop=True),
                         reads=["PT", ("v_b", c)], writes=[("ps", 1)])
                for dc in range(2):
                    S.op("pe", lambda e, dc=dc, c=c: e.matmul(ps[:, 2 + dc, :], kz[:, dc * 128:(dc + 1) * 128],
                                                              v_b[:, c, :], start=True, stop=True),
                         reads=["kz", ("v_b", c)], writes=[("ps", 2 + dc)])
                    S.op("dve", lambda e, dc=dc, h=h: e.scalar_tensor_tensor(
                        St[:, h, dc, :], St[:, h, dc, :], cd[h], ps[:, 2 + dc, :], ALU.mult, ALU.add),
                        reads=[("ps", 2 + dc), "St"], writes=["St"])
                if c < 3 and mode == "B":
                    S.op("pool", lambda e, h=h: e.tensor_copy(Sb[:], St[:, h]), reads=["St"], writes=["Sb"])
                if mode == "B":
                    S.op("act", lambda e: e.activation(go[:], ps[:, 1, :], AF.Square, accum_out=gst[:, 0:1]),
                         reads=[("ps", 1)], writes=["go", "gst0"])
                    P.rstd(gst[:, 0:1], gst[:, 1:2], 1.0 / RDV, rc[:, 16 + h:17 + h], "gst0", "gst1")
                    S.op("dve", lambda e, c=c: e.scalar_tensor_tensor(go[:], ps[:, 1, :], gst[:, 1:2], sg[:, c, :],
                                                                      ALU.mult, ALU.mult),
                         reads=[("ps", 1), "gst1", ("sg", c)], writes=["go"])
                    P.transpose_to(go, 4, lambda k0, n, h=h, c=c: goT[:, h * 4 + k0:h * 4 + k0 + n,
                                                                        c * 128:(c + 1) * 128],
                                   ps_bf, ["go"], [("goT", c)], evac="act")
        if mode == "B":
            w_out_v = w_out.rearrange("(fc p) n -> p fc n", p=128)
            xin_r = x_in.rearrange("(t c p) d -> t p c d", p=128, c=4)
            xout_r = x_out.rearrange("(t c p) d -> t p c d", p=128, c=4)
            for dg in range(4):
                S.dma("sp", xr[:], xin_r[t][:, :, dg * 512:(dg + 1) * 512], writes=["xt"])
                for fb in range(4):
                    wv, wtok = P.load_w([(0, 512, w_out_v[:, fb * 8:(fb + 1) * 8, dg * 512:(dg + 1) * 512])], 8, 512)
                    for jx in range(8):
                        fc = fb * 8 + jx
                        for c in range(4):
                            S.op("pe", lambda e, fc=fc, jx=jx, c=c, wv=wv: e.matmul(
                                ps[:, c, :], goT[:, fc, c * 128:(c + 1) * 128], wv[:, jx, :],
                                start=(fc == 0), stop=(fc == 31)),
                                reads=[wtok, ("goT", c)], writes=[("ps", c)])
                for c in range(4):
                    S.op("dve", lambda e, c=c: e.tensor_tensor(xr[:, c, :], xr[:, c, :], ps[:, c, :], ALU.add),
                         reads=[("ps", c), "xt"], writes=["xt"])
                S.dma("sp", xout_r[t][:, :, dg * 512:(dg + 1) * 512], xr[:], reads=["xt"], writes=[("xout", t, dg)])
    if mode == "A":
        for h in range(RH):
            S.dma("sp", s_out[h].rearrange("(a p) e -> p a e", p=128), St[:, h], reads=["St"], writes=[("sout", h)])
    P.close()


def load_resident(P, dst, pieces_fn, a, nblk, bdim):
    S = P.S
    for b in range(nblk):
        wv, wtok = P.load_w(pieces_fn(b), a, bdim, cast_engs=("act",))
        S.op("pool", lambda e, b=b, wv=wv: e.tensor_copy(dst[:, :, b * bdim:(b + 1) * bdim], wv),
             reads=[wtok], writes=["resident"])


def ones_ssq(P, ps_acc, src_f, sq, ones_f, first, last, rtok, sqtok, acctok):
    S = P.S
    S.op("act", lambda e: e.activation(sq, src_f, AF.Square), reads=[rtok], writes=[sqtok])
    S.op("pe", lambda e: e.matmul(ps_acc, ones_f[:], sq, start=first, stop=last),
         reads=[sqtok, "ones_f"], writes=[acctok])


MH, MNOPE, MROPE, MV, KVR, QR = 16, 128, 64, 128, 512, 1536
SCALE = float((MNOPE + MROPE) ** -0.5)
SEQ = 8192


def rope_small(P, y, outb, cosv, sinv, nh, tmp_a, tmp_b, rd, wr):
    S = P.S
    cb = cosv.unsqueeze(1).broadcast_to([128, nh, 32])
    sb_ = sinv.unsqueeze(1).broadcast_to([128, nh, 32])
    x1, x2 = y[:, :, 0:32], y[:, :, 32:64]
    S.op("dve", lambda e: e.tensor_tensor(tmp_a, x1, cb, ALU.mult), rd, ["rs_a"])
    S.op("pool", lambda e: e.tensor_tensor(tmp_b, x2, sb_, ALU.mult), rd, ["rs_b"])
    S.op("dve", lambda e: e.tensor_tensor(outb[:, :, 0:32], tmp_a, tmp_b, ALU.subtract), ["rs_a", "rs_b"], wr)
    S.op("dve", lambda e: e.tensor_tensor(tmp_a, x2, cb, ALU.mult), rd, ["rs_a"])
    S.op("pool", lambda e: e.tensor_tensor(tmp_b, x1, sb_, ALU.mult), rd, ["rs_b"])
    S.op("dve", lambda e: e.tensor_tensor(outb[:, :, 32:64], tmp_a, tmp_b, ALU.add), ["rs_a", "rs_b"], wr)


def kv_phase(nc, S, cst, x_in, g_row, w_kvd, g_ckv_col, w_kvu, g_kn_col, g_kpe_row, pos_pc,
             KnT, KpeT, V, ntiles=TOK // T):
    P = Phase(nc, S, cst)
    P.setup_stream(4096)
    P.setup_norm(g_row)
    hT = P.sb("hT", [128, KC, T], BF16)
    wkd = P.sb("wkd", [128, KC, 576], BF16)
    wku = P.sb("wku", [128, 4, 4096], BF16)
    ones_f = P.sb("ones_f", [128, 128], F32)
    gcol = P.sb("gcol", [128, 4], F32)
    gkn = P.sb("gkn", [128, 1], F32)
    gpe = P.sb("gpe", [128, 64], F32)
    ckv_f = P.sb("ckv_f", [128, 4, T], F32)
    sq = [P.sb("sq%d" % i, [128, T], F32) for i in range(2)]
    rbc = P.sb("rbc", [128, T], F32)
    ckvT = P.sb("ckvT", [128, 4, T], BF16)
    kn_f = P.sb("kn_f", [128, T], F32)
    knb = P.sb("knb", [128, T], BF16)
    v_b = P.sb("v_b", [128, 2048], BF16)
    kpe_f = P.sb("kpe_f", [128, 64], F32)
    kpe_y = P.sb("kpe_y", [128, 1, 64], F32)
    kpe2 = P.sb("kpe2", [128, 2, 64], BF16)
    kpeT_sb = P.sb("kpeT_sb", [128, T], BF16)
    pst = P.sb("pst", [128, 2], F32)
    ta = P.sb("ta", [128, 1, 32], F32)
    tb_ = P.sb("tb_", [128, 1, 32], F32)
    ps = P.psum("ps", [128, 6, 512], F32)
    ps_bf = P.psum("ps_bf", [128, 2, 1024], BF16)
    S.op("pool", lambda e: e.memset(ones_f[:], 1.0), [], ["ones_f"])
    S.dma("sp", gcol[:], g_ckv_col, writes=["gcol"])
    S.dma("sp", gkn[:], g_kn_col, writes=["gkn"])
    S.dma("sp", gpe[:], g_kpe_row.partition_broadcast(128), writes=["gpe"])
    cos, sin, ctok, stok = P.rope_tables(pos_pc, cst["invf_mla"], 32, TOK // 128, "m")
    wkd_v = w_kvd.rearrange("(kc p) n -> p kc n", p=128)
    wku_v = w_kvu.rearrange("(kc p) n -> p kc n", p=128)
    load_resident(P, wkd, lambda b: [(0, 192, wkd_v[:, :, b * 192:(b + 1) * 192])], KC, 3, 192)
    load_resident(P, wku, lambda b: [(0, 512, wku_v[:, :, b * 512:(b + 1) * 512])], 4, 8, 512)
    xin_v = x_in.rearrange("(t c p) d -> t c p d", p=128, c=4)
    wku_h = wku[:].rearrange("p k (h two d) -> p k h two d", two=2, d=128)
    for t in range(ntiles):
        for c in range(4):
            P.norm_hT(xin_v[t, c], hT, c, ps_bf)
        for oc in range(4):
            bank = oc % 2
            for kc in range(KC):
                S.op("pe", lambda e, kc=kc, oc=oc, bank=bank: e.matmul(
                    ps[:, bank, :], wkd[:, kc, oc * 128:(oc + 1) * 128], hT[:, kc, :],
                    start=(kc == 0), stop=(kc == KC - 1)),
                    reads=["resident"] + [("hT", c) for c in range(4)], writes=[("ps", bank)])
            S.op("act", lambda e, oc=oc, bank=bank: e.copy(ckv_f[:, oc, :], ps[:, bank, :]),
                 reads=[("ps", bank)], writes=[("ckv_f", oc)])
            ones_ssq(P, ps[:, 2, :], ckv_f[:, oc, :], sq[oc % 2][:], ones_f, oc == 0, oc == 3,
                     ("ckv_f", oc), ("sq", oc % 2), ("ps", 2))
        P.rstd(ps[:, 2, :], rbc[:], 1.0 / KVR, EPS, ("ps", 2), "rbc")
        for oc in range(4):
            S.op("dve", lambda e, oc=oc: e.scalar_tensor_tensor(ckvT[:, oc, :], ckv_f[:, oc, :], gcol[:, oc:oc + 1],
                                                                 rbc[:], ALU.mult, ALU.mult),
                 reads=[("ckv_f", oc), "gcol", "rbc"], writes=["ckvT"])
        for c in range(4):
            j = t * 4 + c
            for kc in range(KC):
                S.op("pe", lambda e, kc=kc, c=c: e.matmul(ps[:, 3, 0:64], hT[:, kc, c * 128:(c + 1) * 128],
                                                          wkd[:, kc, 512:576], start=(kc == 0), stop=(kc == KC - 1)),
                     reads=["resident", ("hT", c)], writes=[("ps", 3)])
            S.op("act", lambda e: e.copy(kpe_f[:], ps[:, 3, 0:64]), reads=[("ps", 3)], writes=["kpe_f"])
            S.op("act", lambda e: e.activation(kpe_y[:, 0, :], kpe_f[:], AF.Square, accum_out=pst[:, 0:1]),
                 reads=["kpe_f"], writes=["kpe_y", "pst0"])
            P.rstd(pst[:, 0:1], pst[:, 1:2], 1.0 / MROPE, EPS, "pst0", "pst1")
            S.op("dve", lambda e: e.scalar_tensor_tensor(kpe_y[:, 0, :], kpe_f[:], pst[:, 1:2], gpe[:],
                                                          ALU.mult, ALU.mult),
                 reads=["kpe_f", "pst1", "gpe"], writes=["kpe_y"])
            rope_small(P, kpe_y[:], kpe2[:, 0:1, :], cos[:, j, :], sin[:, j, :], 1, ta[:], tb_[:],
                       ["kpe_y", ctok, stok], ["kpe2"])
            S.op("pool", lambda e: e.tensor_copy(kpe2[:, 1, :], kpe2[:, 0, :]), reads=["kpe2"], writes=["kpe2"])
            P.transpose_to(kpe2[:].rearrange("p a d -> p (a d)"), 1,
                           lambda k0, n, c=c: kpeT_sb[:, c * 128:(c + 1) * 128].unsqueeze(1),
                           ps_bf, ["kpe2"], ["kpeT_sb"], evac="act")
        S.dma("sp", KpeT[:, t * T:(t + 1) * T], kpeT_sb[:], reads=["kpeT_sb"], writes=[("KpeT", t)])
        for h in range(MH):
            bank = h % 2
            for kc in range(4):
                S.op("pe", lambda e, kc=kc, h=h, bank=bank: e.matmul(
                    ps[:, bank, :], wku_h[:, kc, h, 0, :], ckvT[:, kc, :], start=(kc == 0), stop=(kc == 3)),
                    reads=["resident", "ckvT"], writes=[("ps", bank)])
            S.op("act", lambda e, bank=bank: e.copy(kn_f[:], ps[:, bank, :]), reads=[("ps", bank)], writes=["kn_f"])
            ones_ssq(P, ps[:, 2, :], kn_f[:], sq[h % 2][:], ones_f, True, True, "kn_f", ("sq", h % 2), ("ps", 2))
            P.rstd(ps[:, 2, :], rbc[:], 1.0 / MNOPE, EPS, ("ps", 2), "rbc")
            S.op("dve", lambda e: e.scalar_tensor_tensor(knb[:], kn_f[:], gkn[:, 0:1], rbc[:], ALU.mult, ALU.mult),
                 reads=["kn_f", "gkn", "rbc"], writes=["knb"])
            S.dma("sp", KnT[h][:, t * T:(t + 1) * T], knb[:], reads=["knb"], writes=[("KnT", h, t)])
        for c in range(4):
            for hg in range(4):
                bank = 4 + hg % 2
                for hh in range(4):
                    for kc in range(4):
                        S.op("pe", lambda e, kc=kc, c=c, hg=hg, hh=hh, bank=bank: e.matmul(
                            ps[:, bank, hh * 128:(hh + 1) * 128], ckvT[:, kc, c * 128:(c + 1) * 128],
                            wku_h[:, kc, hg * 4 + hh, 1, :], start=(kc == 0), stop=(kc == 3)),
                            reads=["resident", "ckvT"], writes=[("ps", bank)])
                S.op("act", lambda e, hg=hg, bank=bank: e.copy(v_b[:, hg * 512:(hg + 1) * 512], ps[:, bank, :]),
                     reads=[("ps", bank)], writes=["v_b"])
            S.dma("sp", V[t * T + c * 128:t * T + (c + 1) * 128, :], v_b[:], reads=["v_b"], writes=[("V", t, c)])
    P.close()


def mla_layer(nc, S, cst, x_in, x_out, g_row, w_dq, g_q_col, w_uq, g_qn_col, g_qpe_row, w_o, pos_pc,
              qpos_row, KnT_all, KpeT_all, V_all, ntiles=TOK // T):
    outer = ExitStack()
    Phase.uid[0] += 1
    osfx = "_o%d" % Phase.uid[0]
    qnT = outer.enter_context(nc.sbuf_tensor("qnT" + osfx, [128, MH, T], BF16))
    qpeT = outer.enter_context(nc.sbuf_tensor("qpeT" + osfx, [128, 8, T], BF16))
    aoT = outer.enter_context(nc.sbuf_tensor("aoT" + osfx, [128, MH, T], BF16))
    xin_v = x_in.rearrange("(t c p) d -> t c p d", p=128, c=4)
    w_dq_v = w_dq.rearrange("(kc p) n -> p kc n", p=128)
    w_uq_v = w_uq.rearrange("(kc p) n -> p kc n", p=128)
    w_o_v = w_o.rearrange("(fc p) n -> p fc n", p=128)
    for t in range(ntiles):
        P = Phase(nc, S, cst)
        P.setup_stream(3072)
        P.setup_norm(g_row)
        hT = P.sb("hT", [128, KC, T], BF16)
        ones_f = P.sb("ones_f", [128, 128], F32)
        gq = P.sb("gq", [128, 12], F32)
        gqn = P.sb("gqn", [128, 1], F32)
        gqpe = P.sb("gqpe", [128, 64], F32)
        cq_f = P.sb("cq_f", [128, 12, T], F32)
        sq = [P.sb("sq%d" % i, [128, T], F32) for i in range(2)]
        rbc = P.sb("rbc", [128, T], F32)
        cqT = P.sb("cqT", [128, 12, T], BF16)
        qn_f = P.sb("qn_f", [128, T], F32)
        wpe = P.sb("wpe", [128, 12, 1024], BF16)
        qpe_f = P.sb("qpe_f", [128, MH, 64], F32)
        qpe_q = P.sb("qpe_q", [128, MH, 64], F32)
        qpe_b = P.sb("qpe_b", [128, MH, 64], BF16)
        r16 = P.sb("r16", [128, 2, MH], F32)
        ta = P.sb("ta", [128, MH, 32], F32)
        tb_ = P.sb("tb_", [128, MH, 32], F32)
        ps = P.psum("ps", [128, 6, 512], F32)
        ps_bf = P.psum("ps_bf", [128, 2, 1024], BF16)
        S.op("pool", lambda e: e.memset(ones_f[:], 1.0), [], ["ones_f"])
        S.dma("sp", gq[:], g_q_col, writes=["gq"])
        S.dma("sp", gqn[:], g_qn_col, writes=["gqn"])
        S.dma("sp", gqpe[:], g_qpe_row.partition_broadcast(128), writes=["gqpe"])
        cos, sin, ctok, stok = P.rope_tables(pos_pc[:, t * 4:(t + 1) * 4], cst["invf_mla"], 32, 4, "m")
        load_resident(P, wpe, lambda b: [(i * 64, 64, w_uq_v[:, :, (b * 4 + i) * 192 + 128:(b * 4 + i) * 192 + 192])
                                          for i in range(4)], 12, 4, 256)
        for c in range(4):
            P.norm_hT(xin_v[t, c], hT, c, ps_bf)
        for b in range(6):
            for oc2 in range(2):
                oc = b * 2 + oc2
                bank = oc % 2
                for kh in range(2):
                    wv, wtok = P.load_w([(0, 128, w_dq_v[:, kh * 8:(kh + 1) * 8, oc * 128:(oc + 1) * 128])], 8, 128)
                    for k8 in range(8):
                        kc = kh * 8 + k8
                        S.op("pe", lambda e, kc=kc, k8=k8, bank=bank, wv=wv: e.matmul(
                            ps[:, bank, :], wv[:, k8, :], hT[:, kc, :], start=(kc == 0), stop=(kc == KC - 1)),
                            reads=[wtok] + [("hT", c) for c in range(4)], writes=[("ps", bank)])
                S.op("act", lambda e, oc=oc, bank=bank: e.copy(cq_f[:, oc, :], ps[:, bank, :]),
                     reads=[("ps", bank)], writes=[("cq_f", oc)])
                ones_ssq(P, ps[:, 2, :], cq_f[:, oc, :], sq[oc % 2][:], ones_f, oc == 0, oc == 11,
                         ("cq_f", oc), ("sq", oc % 2), ("ps", 2))
        P.rstd(ps[:, 2, :], rbc[:], 1.0 / QR, EPS, ("ps", 2), "rbc")
        for oc in range(12):
            S.op("dve", lambda e, oc=oc: e.scalar_tensor_tensor(cqT[:, oc, :], cq_f[:, oc, :], gq[:, oc:oc + 1],
                                                                 rbc[:], ALU.mult, ALU.mult),
                 reads=[("cq_f", oc), "gq", "rbc"], writes=["cqT"])
        for hp in range(MH // 2):
            wv, wtok = P.load_w([(i * 128, 128, w_uq_v[:, :, (hp * 2 + i) * 192:(hp * 2 + i) * 192 + 128])
                                 for i in range(2)], 12, 256)
            for i in range(2):
                h = hp * 2 + i
                bank = h % 2
                for kc in range(12):
                    S.op("pe", lambda e, kc=kc, i=i, bank=bank, wv=wv: e.matmul(
                        ps[:, bank, :], wv[:, kc, i * 128:(i + 1) * 128], cqT[:, kc, :],
                        start=(kc == 0), stop=(kc == 11)),
                        reads=[wtok, "cqT"], writes=[("ps", bank)])
                S.op("act", lambda e, bank=bank: e.copy(qn_f[:], ps[:, bank, :]), reads=[("ps", bank)], writes=["qn_f"])
                ones_ssq(P, ps[:, 3, :], qn_f[:], sq[h % 2][:], ones_f, True, True, "qn_f", ("sq", h % 2), ("ps", 3))
                P.rstd(ps[:, 3, :], rbc[:], 1.0 / MNOPE, EPS, ("ps", 3), "rbc")
                S.op("dve", lambda e, h=h: e.scalar_tensor_tensor(qnT[:, h, :], qn_f[:], gqn[:, 0:1], rbc[:],
                                                                   ALU.mult, ALU.mult),
                     reads=["qn_f", "gqn", "rbc"], writes=["qnT"])
        for c in range(4):
            for g in range(2):
                for kc in range(12):
                    S.op("pe", lambda e, kc=kc, c=c, g=g: e.matmul(
                        ps[:, 4 + g, :], cqT[:, kc, c * 128:(c + 1) * 128], wpe[:, kc, g * 512:(g + 1) * 512],
                        start=(kc == 0), stop=(kc == 11)),
                        reads=["resident", "cqT"], writes=[("ps", 4 + g)])
                S.op("act", lambda e, g=g: e.copy(qpe_f[:, g * 8:(g + 1) * 8, :].rearrange("p h d -> p (h d)"),
                                                  ps[:, 4 + g, :]),
                     reads=[("ps", 4 + g)], writes=["qpe_f"])
            S.op("act", lambda e: e.activation(qpe_q[:].rearrange("p h d -> p (h d)"),
                                               qpe_f[:].rearrange("p h d -> p (h d)"), AF.Square),
                 reads=["qpe_f"], writes=["qpe_q"])
            S.op("dve", lambda e: e.tensor_reduce(r16[:, 0, :], qpe_q[:], AX.X, ALU.add),
                 reads=["qpe_q"], writes=["r16a"])
            P.rstd(r16[:, 0, :], r16[:, 1, :], 1.0 / MROPE, EPS, "r16a", "r16b")
            S.op("dve", lambda e: e.tensor_tensor(qpe_q[:], qpe_f[:],
                                                  r16[:, 1, :].unsqueeze(2).broadcast_to([128, MH, 64]), ALU.mult),
                 reads=["qpe_f", "r16b"], writes=["qpe_q"])
            S.op("dve", lambda e: e.tensor_tensor(qpe_q[:], qpe_q[:],
                                                  gqpe[:].unsqueeze(1).broadcast_to([128, MH, 64]), ALU.mult),
                 reads=["qpe_q", "gqpe"], writes=["qpe_q"])
            rope_small(P, qpe_q[:], qpe_b[:], cos[:, c, :], sin[:, c, :], MH, ta[:], tb_[:],
                       ["qpe_q", ctok, stok], ["qpe_b"])
            P.transpose_to(qpe_b[:].rearrange("p h d -> p (h d)"), 8,
                           lambda k0, n, c=c: qpeT[:, k0:k0 + n, c * 128:(c + 1) * 128],
                           ps_bf, ["qpe_b"], ["qpeT"], evac="act")
        P.close()

        nk = 16 * (t + 1)
        mask_from = 16 * t
        P = Phase(nc, S, cst)
        kpe = P.sb("kpe", [128, SEQ], BF16)
        NB = 3
        kbuf = [P.sb("kbuf%d" % i, [128, 2048], BF16) for i in range(NB)]
        vbuf = [P.sb("vbuf%d" % i, [128, 16, 128], BF16) for i in range(NB)]
        ebuf = [P.sb("ebuf%d" % i, [128, T], BF16) for i in range(3)]
        pbuf = [P.sb("pbuf%d" % i, [128, T], BF16) for i in range(3)]
        qpos = P.sb("qpos", [128, T], F32)
        kposc = P.sb("kposc", [128, 64], F32)
        ones_b = P.sb("ones_b", [128, 128], BF16)
        rec = P.sb("rec", [128, T], F32)
        ps = P.psum("ps", [128, 7, 512], F32)
        S.op("pool", lambda e: e.memset(ones_b[:], 1.0), [], ["ones_b"])
        S.dma("sp", qpos[:], qpos_row[t * T:(t + 1) * T].partition_broadcast(128), writes=["qpos"])
        S.dma("sp", kposc[:], cst["kposc"], writes=["kposc"])
        for q4 in range(t + 1):
            S.dma("sp", kpe[:, q4 * 2048:(q4 + 1) * 2048], KpeT_all[:, q4 * 2048:(q4 + 1) * 2048], writes=["kpe"])
        steps = [(h, kc) for h in range(MH) for kc in range(nk)]
        nload = [0]
        loaded = {}

        def ensure_loaded(h, g):
            if (h, g) in loaded:
                return loaded[(h, g)]
            bb = nload[0] % NB
            nload[0] += 1
            S.dma("sp", kbuf[bb][:], KnT_all[h][:, g * 2048:(g + 1) * 2048], writes=[("kbuf", bb)])
            S.dma("sp", vbuf[bb][:], V_all[g * 2048:(g + 1) * 2048, h * 128:(h + 1) * 128].rearrange(
                "(i p) d -> p i d", p=128), writes=[("vbuf", bb)])
            loaded[(h, g)] = bb
            return bb

        def emit_S(idx):
            h, kc = steps[idx]
            g, i = kc // 16, kc % 16
            bb = ensure_loaded(h, g)
            p0 = 64 * (h % 2)
            sb_ = idx % 3
            S.op("pe", lambda e: e.matmul(ps[:, sb_, :], kbuf[bb][:, i * 128:(i + 1) * 128], qnT[:, h, :],
                                          start=True, stop=False),
                 reads=[("kbuf", bb), "qnT"], writes=[("ps", sb_)])
            S.op("pe", lambda e: e.matmul(ps[:, sb_, :], kpe[p0:p0 + 64, kc * 128:(kc + 1) * 128],
                                          qpeT[p0:p0 + 64, h // 2, :], start=False, stop=True),
                 reads=["kpe", "qpeT"], writes=[("ps", sb_)])

        def emit_E(idx):
            h, kc = steps[idx]
            sb_ = idx % 3
            if kc >= mask_from:
                S.op("act", lambda e: e.activation(ebuf[sb_][:], ps[:, sb_, :], AF.Exp, scale=SCALE),
                     reads=[("ps", sb_)], writes=[("ebuf", sb_)])
                S.op("dve", lambda e: e.scalar_tensor_tensor(pbuf[sb_][:], qpos[:], kposc[:, kc:kc + 1], ebuf[sb_][:],
                                                             ALU.is_ge, ALU.mult),
                     reads=[("ebuf", sb_), "qpos", "kposc"], writes=[("pbuf", sb_)])
            else:
                S.op("act", lambda e: e.activation(pbuf[sb_][:], ps[:, sb_, :], AF.Exp, scale=SCALE),
                     reads=[("ps", sb_)], writes=[("pbuf", sb_)])

        def emit_PV(idx):
            h, kc = steps[idx]
            g, i = kc // 16, kc % 16
            bb = loaded[(h, g)]
            sb_ = idx % 3
            ao, so = (3, 4) if h % 2 == 0 else (5, 6)
            S.op("pe", lambda e: e.matmul(ps[:, ao, :], vbuf[bb][:, i, :], pbuf[sb_][:],
                                          start=(kc == 0), stop=(kc == nk - 1)),
                 reads=[("vbuf", bb), ("pbuf", sb_)], writes=[("ps", ao)])
            S.op("pe", lambda e: e.matmul(ps[:, so, :], ones_b[:], pbuf[sb_][:],
                                          start=(kc == 0), stop=(kc == nk - 1)),
                 reads=["ones_b", ("pbuf", sb_)], writes=[("ps", so)])
            if kc == nk - 1:
                S.op("dve", lambda e: e.reciprocal(rec[:], ps[:, so, :]), reads=[("ps", so)], writes=["rec"])
                S.op("dve", lambda e: e.tensor_tensor(aoT[:, h, :], ps[:, ao, :], rec[:], ALU.mult),
                     reads=[("ps", ao), "rec"], writes=["aoT"])

        n = len(steps)
        emit_S(0)
        if n > 1:
            emit_S(1)
        for idx in range(n):
            if idx + 2 < n:
                emit_S(idx + 2)
            emit_E(idx)
            emit_PV(idx)
        P.close()

        P = Phase(nc, S, cst)
        P.setup_stream(4096)
        xr = P.sb("xr", [128, 4, 512], F32)
        ps = P.psum("ps", [128, 6, 512], F32)
        xin_r = x_in.rearrange("(t c p) d -> t p c d", p=128, c=4)
        xout_r = x_out.rearrange("(t c p) d -> t p c d", p=128, c=4)
        for dg in range(4):
            S.dma("sp", xr[:], xin_r[t][:, :, dg * 512:(dg + 1) * 512], writes=["xr"])
            for fb in range(2):
                wv, wtok = P.load_w([(0, 512, w_o_v[:, fb * 8:(fb + 1) * 8, dg * 512:(dg + 1) * 512])], 8, 512)
                for jx in range(8):
                    fc = fb * 8 + jx
                    for c in range(4):
                        S.op("pe", lambda e, fc=fc, jx=jx, c=c, wv=wv: e.matmul(
                            ps[:, c, :], aoT[:, fc, c * 128:(c + 1) * 128], wv[:, jx, :],
                            start=(fc == 0), stop=(fc == MH - 1)),
                            reads=[wtok, "aoT"], writes=[("ps", c)])
            for c in range(4):
                S.op("dve", lambda e, c=c: e.tensor_tensor(xr[:, c, :], xr[:, c, :], ps[:, c, :], ALU.add),
                     reads=[("ps", c), "xr"], writes=["xr"])
            S.dma("sp", xout_r[t][:, :, dg * 512:(dg + 1) * 512], xr[:], reads=["xr"], writes=[("xout", t, dg)])
        P.close()
    outer.close()


def host_consts():
    rc, cd, mask, lg = ret_consts()
    invf_ret = (1.0 / (10000.0 ** (np.arange(0, 256, 2, dtype=np.float32) / np.float32(256)))).astype(np.float32)
    invf_mla = (1.0 / (10000.0 ** (np.arange(0, 64, 2, dtype=np.float32) / np.float32(64)))).astype(np.float32)
    kposc = (np.arange(64, dtype=np.float32)[None, :] * 128 + np.arange(128, dtype=np.float32)[:, None])
    return {
        "ident": np.eye(128, dtype=np.float32),
        "rc": rc,
        "mask": mask,
        "invf_ret": np.ascontiguousarray(np.broadcast_to(invf_ret[None, :], (128, 128))),
        "invf_mla": np.ascontiguousarray(np.broadcast_to(invf_mla[None, :], (128, 32))),
        "kposc": np.ascontiguousarray(kposc.astype(np.float32)),
    }, lg


class Prog:
    def __init__(self):
        self.nc = bass.Bass("TRN2", target_bir_lowering=False)
        self.es = ExitStack()
        self.S = Sched(self.nc, self.es)
        self.ins = {}

    def inp(self, name, arr_example):
        dt = {"float32": F32, "int32": I32, "bfloat16": BF16}[str(arr_example.dtype)]
        ap = self.nc.dram_tensor(name, list(arr_example.shape), dt, kind="ExternalInput").ap()
        self.ins[name] = ap
        return ap

    def out(self, name, shape, dt):
        return self.nc.dram_tensor(name, list(shape), dt, kind="ExternalOutput").ap()

    def consts(self, cst_np):
        return {k: self.inp("c_" + k, v) for k, v in cst_np.items()}

    def run(self, in_maps):
        self.es.close()
        res = run_bass_kernel_spmd(self.nc, in_maps, core_ids=list(range(NCORE)))
        return res.results


def _cin(cst_np):
    return {"c_" + k: v for k, v in cst_np.items()}


def coef_table(lg):
    out = []
    for c in range(NCORE):
        r = c % 4
        t = np.zeros((4, RH), np.float64)
        for rp in range(r):
            t[rp] = np.exp(lg * (2048.0 * (r - rp - 1)))
        out.append(t.reshape(-1).astype(np.float32))
    return out


def launch_ret_a(cst_np, x_sh, pos_pc, g, w_in):
    pr = Prog()
    cst = pr.consts(cst_np)
    x = pr.inp("x", x_sh[0]); pos = pr.inp("pos", pos_pc[0]); ga = pr.inp("g", g); wi = pr.inp("w_in", w_in)
    so = pr.out("s_out", [RH, RDK, RDV], F32)
    ret_phase(pr.nc, pr.S, cst, "A", x, None, ga, wi, None, pos, None, None, so, 4)
    maps = [dict(_cin(cst_np), x=x_sh[c], pos=pos_pc[c], g=g, w_in=w_in) for c in range(NCORE)]
    res = pr.run(maps)
    return [r["s_out"] for r in res]


def gather_states(s_loc):
    return [np.ascontiguousarray(np.stack([s_loc[(c // 4) * 4 + r] for r in range(4)])) for c in range(NCORE)]


def launch_ret_b_mlp(cst_np, coefs, x_sh, pos_pc, g, w_in, w_out, s_all, g2, w1, w2, nxt):
    pr = Prog()
    cst = pr.consts(cst_np)
    x = pr.inp("x", x_sh[0]); pos = pr.inp("pos", pos_pc[0]); ga = pr.inp("g", g); wi = pr.inp("w_in", w_in)
    wo = pr.inp("w_out", w_out); sa = pr.inp("s_all", s_all[0]); cf = pr.inp("coef", coefs[0])
    g2a = pr.inp("g2", g2); w1a = pr.inp("w1", w1); w2a = pr.inp("w2", w2)
    xo = pr.out("xo", [TOK, D], F32)
    ret_phase(pr.nc, pr.S, cst, "B", x, xo, ga, wi, wo, pos, sa, cf, None, 4)
    mlp_phase(pr.nc, pr.S, cst, xo, xo, g2a, w1a, w2a)
    extra = {}
    outs = ["xo"]
    if nxt[0] == "ret_a":
        g3 = pr.inp("g3", nxt[1]); wi3 = pr.inp("w_in3", nxt[2])
        so = pr.out("s_out", [RH, RDK, RDV], F32)
        ret_phase(pr.nc, pr.S, cst, "A", xo, None, g3, wi3, None, pos, None, None, so, 4)
        extra = {"g3": nxt[1], "w_in3": nxt[2]}
        outs.append("s_out")
    else:
        kv = nxt[1]
        aps = {k: pr.inp("kv_" + k, v) for k, v in kv.items()}
        KnT = pr.out("KnT", [MH, 128, TOK], BF16)
        KpeT = pr.out("KpeT", [128, TOK], BF16)
        V = pr.out("V", [TOK, MH * MV], BF16)
        kv_phase(pr.nc, pr.S, cst, xo, aps["g"], aps["w_kvd"], aps["g_ckv_col"], aps["w_kvu"], aps["g_kn_col"],
                 aps["g_kpe_row"], pos, KnT, KpeT, V)
        extra = {"kv_" + k: v for k, v in kv.items()}
        outs += ["KnT", "KpeT", "V"]
    maps = [dict(_cin(cst_np), x=x_sh[c], pos=pos_pc[c], g=g, w_in=w_in, w_out=w_out, s_all=s_all[c],
                 coef=coefs[c], g2=g2, w1=w1, w2=w2, **extra) for c in range(NCORE)]
    res = pr.run(maps)
    return {k: [r[k] for r in res] for k in outs}


def launch_mla(cst_np, x_sh, pos_pc, qpos, kn_all, kpe_all, v_all, layers):
    pr = Prog()
    cst = pr.consts(cst_np)
    x = pr.inp("x", x_sh[0]); pos = pr.inp("pos", pos_pc[0]); qp = pr.inp("qpos", qpos[0])
    kn = pr.inp("kn", kn_all[0]); kp = pr.inp("kp", kpe_all[0]); va = pr.inp("va", v_all[0])
    xo = pr.out("xo", [TOK, D], F32)
    shared = {}
    src = x
    for li, L in enumerate(layers):
        a = {k: pr.inp("L%d_%s" % (li, k), v) for k, v in L.items()}
        shared.update({"L%d_%s" % (li, k): v for k, v in L.items()})
        mla_layer(pr.nc, pr.S, cst, src, xo, a["g"], a["w_dq"], a["g_q_col"], a["w_uq"], a["g_qn_col"],
                  a["g_qpe_row"], a["w_o"], pos, qp, kn, kp, va)
        mlp_phase(pr.nc, pr.S, cst, xo, xo, a["g2"], a["w1"], a["w2"])
        src = xo
    maps = [dict(_cin(cst_np), x=x_sh[c], pos=pos_pc[c], qpos=qpos[c], kn=kn_all[c], kp=kpe_all[c], va=v_all[c],
                 **shared) for c in range(NCORE)]
    res = pr.run(maps)
    return [r["xo"] for r in res]


def col_layout(v):
    v = np.asarray(v, np.float32)
    return np.ascontiguousarray(v.reshape(-1, 128).T)


def kernel(x, positions, norm_mix, norm_mlp, ret_w_in, ret_w_out, kv_norm_in,
           mla_w_kv_down, mla_kv_norm, mla_w_kv_up, mla_k_nope_norm, mla_k_pe_norm,
           mla_w_dq, mla_q_norm, mla_w_uq, mla_q_nope_norm, mla_q_pe_norm, mla_w_o,
           mlp_w1, mlp_w2):
    f = lambda a: np.ascontiguousarray(np.asarray(a, dtype=np.float32))
    cst_np, lg = host_consts()
    coefs = coef_table(lg)
    x = f(x)
    positions = np.asarray(positions, dtype=np.int32)
    x_sh = [np.ascontiguousarray(x[c // 4, (c % 4) * TOK:(c % 4 + 1) * TOK]) for c in range(NCORE)]
    pos_pc = [np.ascontiguousarray(positions[c // 4, (c % 4) * TOK:(c % 4 + 1) * TOK].reshape(16, 128).T)
              for c in range(NCORE)]
    qpos = [np.arange((c % 4) * TOK, (c % 4 + 1) * TOK, dtype=np.float32) for c in range(NCORE)]
    norm_mix, norm_mlp = f(norm_mix), f(norm_mlp)
    ret_w_in, ret_w_out, mlp_w1, mlp_w2 = f(ret_w_in), f(ret_w_out), f(mlp_w1), f(mlp_w2)

    s_loc = launch_ret_a(cst_np, x_sh, pos_pc, norm_mix[0], ret_w_in[0])
    r = launch_ret_b_mlp(cst_np, coefs, x_sh, pos_pc, norm_mix[0], ret_w_in[0], ret_w_out[0], gather_states(s_loc),
                         norm_mlp[0], mlp_w1[0], mlp_w2[0], ("ret_a", norm_mix[1], ret_w_in[1]))
    x_sh = r["xo"]
    kv = {"g": f(kv_norm_in), "w_kvd": f(mla_w_kv_down), "g_ckv_col": col_layout(mla_kv_norm),
          "w_kvu": f(mla_w_kv_up), "g_kn_col": col_layout(mla_k_nope_norm), "g_kpe_row": f(mla_k_pe_norm)}
    r = launch_ret_b_mlp(cst_np, coefs, x_sh, pos_pc, norm_mix[1], ret_w_in[1], ret_w_out[1],
                         gather_states(r["s_out"]), norm_mlp[1], mlp_w1[1], mlp_w2[1], ("kv", kv))
    x_sh = r["xo"]
    kn_all, kpe_all, v_all = [], [], []
    for c in range(NCORE):
        b = c // 4
        kn_all.append(np.ascontiguousarray(np.concatenate([r["KnT"][b * 4 + q] for q in range(4)], axis=2)))
        kpe_all.append(np.ascontiguousarray(np.concatenate([r["KpeT"][b * 4 + q] for q in range(4)], axis=1)))
        v_all.append(np.ascontiguousarray(np.concatenate([r["V"][b * 4 + q] for q in range(4)], axis=0)))
    layers = []
    for j in range(2):
        layers.append({"g": norm_mix[2 + j], "w_dq": f(mla_w_dq[j]), "g_q_col": col_layout(mla_q_norm[j]),
                       "w_uq": f(mla_w_uq[j]), "g_qn_col": col_layout(mla_q_nope_norm[j]),
                       "g_qpe_row": f(mla_q_pe_norm[j]), "w_o": f(mla_w_o[j]),
                       "g2": norm_mlp[2 + j], "w1": mlp_w1[2 + j], "w2": mlp_w2[2 + j]})
    xfull = np.stack([np.concatenate([x_sh[b * 4 + q] for q in range(4)], axis=0) for b in range(2)])
    own = [[r, 7 - r, 8 + r, 15 - r] for r in range(4)]
    x_f, pos_f, qpos_f = [], [], []
    for c in range(NCORE):
        b, r = c // 4, c % 4
        x_f.append(np.ascontiguousarray(np.concatenate([xfull[b, g * T:(g + 1) * T] for g in own[r]], axis=0)))
        pp = np.concatenate([positions[b, g * T:(g + 1) * T] for g in own[r]])
        pos_f.append(np.ascontiguousarray(pp.reshape(16, 128).T))
        qpos_f.append(np.concatenate([np.arange(g * T, (g + 1) * T, dtype=np.float32) for g in own[r]]))
    xo = launch_mla(cst_np, x_f, pos_f, qpos_f, kn_all, kpe_all, v_all, layers)
    out = np.empty((2, SEQ, D), np.float32)
    for c in range(NCORE):
        b, r = c // 4, c % 4
        for i, g in enumerate(own[r]):
            out[b, g * T:(g + 1) * T] = xo[c][i * T:(i + 1) * T]
    return out
```
